# Optimizing a Trainium2 kernel written in Bass

```python
import math
import jax, jax.numpy as jnp
from jax import lax
import numpy as np

D_MODEL = 2048
BATCH = 8
SEQ = 2048
DEPTH = 1

CHUNK = 64
N_META = 16
E_CONV = D_MODEL // 2
E_SSM = D_MODEL // 2
CONV_WIDTH = 31
SSM_GROUP = 16
SSM_STATE = 64
N_SSM_GROUPS = E_SSM // SSM_GROUP
NORM_EPS = 1e-6
LN_EPS = 1e-5
DT_MIN = 1e-3
DT_MAX = 1e-1
IN_SPLITS = (E_CONV, E_CONV, E_CONV, E_SSM, E_SSM, D_MODEL, D_MODEL)
IN_WIDTH = sum(IN_SPLITS)

kernel_name = "gated_conformer_s5_hybrid_block"


def _rmsnorm(x, gain):
    x32 = x.astype(jnp.float32)
    y = x32 * lax.rsqrt(jnp.mean(x32 * x32, axis=-1, keepdims=True) + NORM_EPS)
    return (y * gain.astype(jnp.float32)).astype(x.dtype)


def _layernorm(x, gain, bias):
    x32 = x.astype(jnp.float32)
    mu = jnp.mean(x32, axis=-1, keepdims=True)
    var = jnp.mean(jnp.square(x32 - mu), axis=-1, keepdims=True)
    y = (x32 - mu) * lax.rsqrt(var + LN_EPS)
    return (y * gain.astype(jnp.float32) + bias.astype(jnp.float32)).astype(x.dtype)


def _split_columns(p):
    idx = np.cumsum(IN_SPLITS)[:-1].tolist()
    return jnp.split(p, idx, axis=-1)


def _conformer_conv(v, g, z, dw_w, dw_b, ln_g, ln_b):
    a = v * jax.nn.sigmoid(g)
    a = lax.conv_general_dilated(
        a, dw_w.astype(a.dtype), window_strides=(1,),
        padding=[(CONV_WIDTH - 1, 0)],
        dimension_numbers=("NWC", "WIO", "NWC"),
        feature_group_count=E_CONV) + dw_b
    a = _layernorm(a, ln_g, ln_b)
    a = jax.nn.silu(a)
    return a * jax.nn.silu(z)


def _s5_scan(u, lam_re, lam_im, log_step, b_re, b_im, c_re, c_im, d_skip):
    bsz, L, _ = u.shape
    u32 = u.astype(jnp.float32).reshape(bsz, L, N_SSM_GROUPS, SSM_GROUP)
    lam = lax.complex(jnp.minimum(lam_re.astype(jnp.float32), -1e-4),
                      lam_im.astype(jnp.float32))
    step = jnp.exp(log_step.astype(jnp.float32))[:, None]
    lam_bar = jnp.exp(lam * step)
    b_mat = lax.complex(b_re.astype(jnp.float32), b_im.astype(jnp.float32))
    b_bar = ((lam_bar - 1.0) / lam)[..., None] * b_mat
    bu = jnp.einsum("gph,blgh->blgp", b_bar, u32.astype(jnp.complex64))
    a = jnp.broadcast_to(lam_bar, bu.shape)

    def combine(left, right):
        a_l, b_l = left
        a_r, b_r = right
        return a_r * a_l, a_r * b_l + b_r

    _, states = lax.associative_scan(combine, (a, bu), axis=1)
    c_mat = lax.complex(c_re.astype(jnp.float32), c_im.astype(jnp.float32))
    y = jnp.real(jnp.einsum("ghp,blgp->blgh", c_mat, states))
    y = y + d_skip.astype(jnp.float32).reshape(N_SSM_GROUPS, SSM_GROUP) * u32
    return y.reshape(bsz, L, E_SSM).astype(u.dtype)


def setup_inputs(seed: int = 0) -> dict:
    key = jax.random.key(seed)
    ks = jax.random.split(key, 24)
    f32 = jnp.float32
    nrm = lambda k, shape, scale: jax.random.normal(k, shape, f32) * scale
    n_idx = jnp.arange(SSM_STATE, dtype=f32)
    lam_re = -0.5 + 0.01 * jax.random.normal(ks[10], (N_SSM_GROUPS, SSM_STATE), f32)
    lam_im = math.pi * n_idx[None, :] + 0.01 * jax.random.normal(ks[11], (N_SSM_GROUPS, SSM_STATE), f32)
    log_step = jax.random.uniform(ks[12], (N_SSM_GROUPS,), f32,
                                  math.log(DT_MIN), math.log(DT_MAX))
    return {
        "x": nrm(ks[0], (BATCH, SEQ, D_MODEL), 1.0),
        "meta": nrm(ks[1], (N_META, D_MODEL), 1.0),
        "norm_g": 1.0 + nrm(ks[2], (D_MODEL,), 0.01),
        "w_in": nrm(ks[3], (D_MODEL, IN_WIDTH), D_MODEL ** -0.5),
        "b_gate": nrm(ks[4], (2 * D_MODEL,), 0.01),
        "dw_w": nrm(ks[5], (CONV_WIDTH, 1, E_CONV), CONV_WIDTH ** -0.5),
        "dw_b": nrm(ks[6], (E_CONV,), 0.01),
        "ln_g": 1.0 + nrm(ks[7], (E_CONV,), 0.01),
        "ln_b": nrm(ks[8], (E_CONV,), 0.01),
        "w_conv": nrm(ks[9], (E_CONV, D_MODEL), E_CONV ** -0.5),
        "lam_re": lam_re,
        "lam_im": lam_im,
        "log_step": log_step,
        "b_re": nrm(ks[13], (N_SSM_GROUPS, SSM_STATE, SSM_GROUP), (2 * SSM_GROUP) ** -0.5),
        "b_im": nrm(ks[14], (N_SSM_GROUPS, SSM_STATE, SSM_GROUP), (2 * SSM_GROUP) ** -0.5),
        "c_re": nrm(ks[15], (N_SSM_GROUPS, SSM_GROUP, SSM_STATE), (2 * SSM_STATE) ** -0.5),
        "c_im": nrm(ks[16], (N_SSM_GROUPS, SSM_GROUP, SSM_STATE), (2 * SSM_STATE) ** -0.5),
        "d_skip": nrm(ks[17], (E_SSM,), 1.0),
        "w_glu": nrm(ks[18], (E_SSM, E_SSM), E_SSM ** -0.5),
        "b_glu": nrm(ks[19], (E_SSM,), 0.01),
        "w_ssm": nrm(ks[20], (E_SSM, D_MODEL), E_SSM ** -0.5),
        "w_out": nrm(ks[21], (D_MODEL, D_MODEL), D_MODEL ** -0.5),
        "final_g": 1.0 + nrm(ks[22], (D_MODEL,), 0.01),
    }


def reference(x, meta, norm_g, w_in, b_gate, dw_w, dw_b, ln_g, ln_b, w_conv,
              lam_re, lam_im, log_step, b_re, b_im, c_re, c_im, d_skip,
              w_glu, b_glu, w_ssm, w_out, final_g):
    bsz = x.shape[0]
    h = jnp.concatenate(
        [jnp.broadcast_to(meta[None].astype(x.dtype), (bsz, N_META, D_MODEL)), x], axis=1)
    for _ in range(DEPTH):
        xn = _rmsnorm(h, norm_g)
        proj = jnp.einsum("bld,de->ble", xn, w_in)
        v_c, g_c, z_c, u_s, z_s, gl_c, gl_s = _split_columns(proj)
        y_c = _conformer_conv(v_c, g_c, z_c, dw_w, dw_b, ln_g, ln_b)
        y_s = _s5_scan(u_s, lam_re, lam_im, log_step, b_re, b_im, c_re, c_im, d_skip)
        y_s = jax.nn.gelu(y_s)
        y_s = y_s * jax.nn.sigmoid(jnp.einsum("ble,ef->blf", y_s, w_glu) + b_glu)
        y_s = y_s * jax.nn.silu(z_s)
        gate_c = jax.nn.sigmoid(gl_c + b_gate[:D_MODEL])
        gate_s = jax.nn.sigmoid(gl_s + b_gate[D_MODEL:])
        merged = (gate_c * jnp.einsum("ble,ed->bld", y_c, w_conv)
                  + gate_s * jnp.einsum("ble,ed->bld", y_s, w_ssm))
        h = h + jnp.einsum("bld,de->ble", merged, w_out)
    out = _rmsnorm(h, final_g)
    return out[:, N_META:, :]
```

```python
import math
from contextlib import ExitStack

import numpy as np
import concourse.bass as bass
import concourse.mybir as mybir
from concourse.bass_utils import run_bass_kernel_spmd

F32 = mybir.dt.float32
BF16 = mybir.dt.bfloat16
I32 = mybir.dt.int32
AF = mybir.ActivationFunctionType
ALU = mybir.AluOpType

D = 2048
SEQ = 2048
NM = 16
L = SEQ + NM
NCORES = 8
E = 1024
CW = 31
PADL = CW - 1
NORM_EPS = 1e-6
LN_EPS = 1e-5
SMW = 4352
MAGIC = 12582912.0
TWO_PI = 2.0 * math.pi
GELU_C = 2.0 * math.sqrt(2.0 / math.pi)

TT_ALL = [(0, NM)] + [(NM + 512 * i, 512) for i in range(4)]
TT_REAL = TT_ALL[1:]

C_V, C_G, C_ZC, C_U, C_ZS, C_GLC, C_GLS = 0, 1024, 2048, 3072, 4096, 5120, 7168


SAME_ENGINE_WAITS = ("pool", "act", "dve")
S5ENG = "dve"


class _Stop(Exception):
    pass


class Src:
    def __init__(self, sem, step):
        self.sem, self.step, self.count = sem, step, 0


class Buf:
    __slots__ = ("w", "r")

    def __init__(self):
        self.w = None
        self.r = []


def alias_from(new_bufs, old_bufs):
    evs = []
    for b in old_bufs:
        if b.w is not None:
            evs.append(b.w)
        evs.extend(b.r)
    for nb in new_bufs:
        nb.r = list(nb.r) + evs


class Sched:
    def __init__(self):
        self.eng = {}

    def add_engine(self, name, sem):
        self.eng[name] = dict(src=Src(sem, 1), insts=[], waited={})

    def op(self, eng, fn, reads=(), writes=(), signal=True, src=None):
        En = self.eng[eng]
        s = src if src is not None else En["src"]
        need = {}
        srcs = {}
        deps = []
        for b in reads:
            if b.w is not None:
                deps.append(b.w)
        for b in writes:
            if b.w is not None:
                deps.append(b.w)
            deps.extend(b.r)
        for (ds, v) in deps:
            if ds is En["src"] and (eng == "pe" or eng not in SAME_ENGINE_WAITS):
                continue
            k = id(ds)
            if v > need.get(k, 0):
                need[k] = v
                srcs[k] = ds
        waits = []
        for k, v in need.items():
            if En["waited"].get(k, 0) >= v:
                continue
            En["waited"][k] = v
            waits.append((srcs[k].sem, v))
        if signal:
            s.count += s.step
            ev = (s, s.count)
            sig = (s.sem, s.step)
        else:
            ev = (s, s.count + s.step)
            sig = None
        En["insts"].append((waits, fn, sig))
        for b in reads:
            b.r.append(ev)
        for b in writes:
            b.w = ev
            b.r = []
        return ev

    def wait_all(self, eng, events):
        En = self.eng[eng]
        waits = []
        for (ds, v) in events:
            k = id(ds)
            if En["waited"].get(k, 0) >= v:
                continue
            En["waited"][k] = v
            waits.append((ds.sem, v))
        En["insts"].append((waits, None, None))

    def emit(self, name, e):
        for waits, fn, sig in self.eng[name]["insts"]:
            for sem, v in waits:
                e.wait_ge(sem, v)
            if fn is None:
                continue
            ins = fn(e)
            if sig is not None:
                ins.then_inc(sig[0], sig[1])


def build(dbg=None):
    nc = bass.Bass("TRN2", target_bir_lowering=False)
    S = Sched()

    def din(name, shape):
        return nc.dram_tensor(name, list(shape), F32, kind="ExternalInput").ap()

    x = din("x", [SEQ, D])
    meta = din("meta", [NM, D])
    norm_g = din("norm_g", [D])
    w_in = din("w_in", [D, 9216])
    b_gate = din("b_gate", [4096])
    dw_w = din("dw_w", [CW, 1, E])
    dw_b = din("dw_b", [E])
    ln_g = din("ln_g", [E])
    ln_b = din("ln_b", [E])
    w_conv = din("w_conv", [E, D])
    lam_re = din("lam_re", [64, 64])
    lam_im = din("lam_im", [64, 64])
    log_step = din("log_step", [64])
    b_re = din("b_re", [64, 64, 16])
    b_im = din("b_im", [64, 64, 16])
    c_re = din("c_re", [64, 16, 64])
    c_im = din("c_im", [64, 16, 64])
    d_skip = din("d_skip", [E])
    w_glu = din("w_glu", [E, E])
    b_glu = din("b_glu", [E])
    w_ssm = din("w_ssm", [E, D])
    w_out = din("w_out", [D, D])
    final_g = din("final_g", [D])
    out = nc.dram_tensor("out", [SEQ, D], F32, kind="ExternalOutput").ap()
    ps_scr = nc.dram_tensor("ps_scr", [64, 128, L], BF16).ap()
    dbg_out = None
    if dbg is not None:
        dbg_out = nc.dram_tensor("dbg", list(dbg["shape"]), F32, kind="ExternalOutput").ap()

    es = ExitStack()
    with es:
        def sb(name, shape, dt):
            return es.enter_context(nc.sbuf_tensor(name, list(shape), dt))

        def sem(name):
            return es.enter_context(nc.semaphore(name))

        XM = sb("XM", [128, 16, L], BF16)
        AC = sb("AC", [128, 33264], BF16)
        WS = sb("WS", [128, 4, 4096], BF16)
        TA = sb("TA", [128, 8192], BF16)
        SM = sb("SM", [128, SMW], F32)
        ident = sb("ident", [128, 128], BF16)
        identf = sb("identf", [128, 128], F32)
        ones_b = sb("ones_b", [128, 128], BF16)
        BT = sb("BT", [128, 2, 8, 128], BF16)
        WC = sb("WC", [128, 3, 8, 128], BF16)
        DSKD = sb("DSKD", [128, 8, 128], BF16)
        ps2 = [es.enter_context(nc.psum_tensor(f"ps{i}", [128, 1024], F32)) for i in range(4)]
        ps = [ps2[j // 2][:, (j % 2) * 512:(j % 2 + 1) * 512] for j in range(8)]
        psB = [Buf() for _ in range(8)]

        A_EL = 8 * (L + PADL)
        Aap = AC[:, 0:A_EL]
        Cap = AC[:, A_EL:A_EL + 8 * L]
        ABUF = Aap.rearrange("p (b t) -> p b t", b=8)
        UBUF = ABUF[:, :, PADL:]
        CBUF = Cap.rearrange("p (b t) -> p b t", b=8)
        A_f32 = Aap.bitcast(F32)
        C_f32 = Cap.bitcast(F32)
        AC_f32 = AC[:, :].bitcast(F32)

        for n in ("pe", "act", "dve", "pool", "sp"):
            S.add_engine(n, sem("s_" + n))

        def dsrc(name):
            return Src(sem(name), 16)

        XMB = [[Buf() for _ in range(5)] for _ in range(16)]

        def tidx(t0):
            return 0 if t0 < NM else 1 + (t0 - NM) // 512

        def xm_all(k):
            return tuple(XMB[k])
        WSb = [Buf() for _ in range(4)]
        Ab = [Buf() for _ in range(8)]
        Cb = [Buf() for _ in range(8)]
        TAb = Buf()
        SMb = Buf()
        CONSTb = Buf()
        S5Wb = Buf()

        sm_off = [0]

        def smalloc(n):
            o = sm_off[0]
            sm_off[0] += n
            assert sm_off[0] <= SMW - 1024, sm_off[0]
            return SM[:, o:o + n]

        def act(out_, in_, func, bias=0.0, scale=1.0, accum=None, reads=(), writes=()):
            if accum is None:
                return S.op("act", lambda e: e.activation(out=out_, in_=in_, func=func, bias=bias, scale=scale),
                            reads, writes)
            return S.op("act", lambda e: e.activation(out=out_, in_=in_, func=func, bias=bias, scale=scale,
                                                      accum_out=accum), reads, writes)

        def tt(eng, out_, in0, in1, op, reads=(), writes=()):
            return S.op(eng, lambda e: e.tensor_tensor(out=out_, in0=in0, in1=in1, op=op), reads, writes)

        def ts(eng, out_, in0, s1, s2, op0, op1=None, reads=(), writes=()):
            if op1 is None:
                return S.op(eng, lambda e: e.tensor_scalar(out=out_, in0=in0, scalar1=s1, scalar2=None, op0=op0),
                            reads, writes)
            return S.op(eng, lambda e: e.tensor_scalar(out=out_, in0=in0, scalar1=s1, scalar2=s2, op0=op0, op1=op1),
                        reads, writes)

        def stt(out_, in0, scalar, in1, op0, op1, reads=(), writes=()):
            return S.op("dve", lambda e: e.scalar_tensor_tensor(out=out_, in0=in0, scalar=scalar, in1=in1,
                                                                op0=op0, op1=op1), reads, writes)

        def cp(eng, out_, in_, reads=(), writes=()):
            if eng == "act":
                return S.op("act", lambda e: e.copy(out=out_, in_=in_), reads, writes)
            return S.op(eng, lambda e: e.tensor_copy(out=out_, in_=in_), reads, writes)

        def mm(out_, lhsT, rhs, start, stop, reads=(), writes=(), signal=False, tile_position=None):
            if tile_position is None:
                return S.op("pe", lambda e: e.matmul(out_, lhsT, rhs, start=start, stop=stop),
                            reads, writes, signal=signal)
            return S.op("pe", lambda e: e.matmul(out_, lhsT, rhs, start=start, stop=stop,
                                                 tile_position=tile_position), reads, writes, signal=signal)

        def transpose(out_, in_, idn, reads=(), writes=(), signal=False):
            return S.op("pe", lambda e: e.transpose(out_, in_, idn), reads, writes, signal=signal)

        def dma(q, out_, in_, src, reads=(), writes=(), nonc=False):
            if nonc:
                return S.op(q, lambda e: e.dma_start(out=out_, in_=in_, allow_slow_non_contiguous=True),
                            reads, writes, src=src)
            return S.op(q, lambda e: e.dma_start(out=out_, in_=in_), reads, writes, src=src)

        bank_rr = [0]

        def next_bank(pool=(0, 1, 2, 3, 4, 5, 6, 7)):
            b = pool[bank_rr[0] % len(pool)]
            bank_rr[0] += 1
            return b

        def dbg_finish(bufs):
            d_dbg2 = dsrc("d_dbgF")
            dma("sp", out[0:128, 0:128], identf[:, :], d_dbg2, (CONSTb,), ())
            S.wait_all("sp", [(d_dbg2, d_dbg2.count)])
            raise _Stop()

        def dbg_dump(nblk, view_fn, bufs_fn, tmpf, tmp_bufs):
            evs = []
            for m in range(nblk):
                cp("dve", tmpf, view_fn(m), reads=tuple(bufs_fn(m)), writes=tuple(tmp_bufs))
                evs.append(dma("sp", dbg_out[m], tmpf, dsrc(f"d_dbg{m}"), tuple(tmp_bufs), ()))
            S.wait_all("sp", evs[-1:])
            dbg_finish(None)

        def program():
            par_n = [0]
            deferred_par = []
            parB_sm = []
            parB_ab = []

            def pbuf(lst):
                b_ = Buf()
                lst.append(b_)
                return b_

            def d_par_new():
                par_n[0] += 1
                return dsrc(f"d_par{par_n[0]}")
            io_jp = smalloc(128)
            S.op("pool", lambda e: e.iota(io_jp, [[1, 128]], base=0, channel_multiplier=-1,
                                          allow_small_or_imprecise_dtypes=True), (), (CONSTb,))
            ts("dve", identf[:, :], io_jp, 0.0, None, ALU.is_equal, reads=(CONSTb,), writes=(CONSTb,))
            cp("dve", ident[:, :], identf[:, :], reads=(CONSTb,), writes=(CONSTb,))
            S.op("pool", lambda e: e.memset(ones_b[:, :], 1.0), (), (CONSTb,))
            iota_t = smalloc(512)
            S.op("pool", lambda e: e.iota(iota_t, [[1, 512]], base=0, channel_multiplier=0,
                                          allow_small_or_imprecise_dtypes=True), (), (CONSTb,))
            mski = smalloc(2).bitcast(I32)
            msk = smalloc(2)
            S.op("pool", lambda e: e.iota(mski[:, 0:1], [[0, 1]], base=0, channel_multiplier=1), (), (CONSTb,))
            ts("dve", mski[:, 0:1], mski[:, 0:1], 4, 1, ALU.arith_shift_right, ALU.bitwise_and,
               reads=(CONSTb,), writes=(CONSTb,))
            cp("dve", msk[:, 1:2], mski[:, 0:1], reads=(CONSTb,), writes=(CONSTb,))
            ts("dve", msk[:, 0:1], msk[:, 1:2], -1.0, 1.0, ALU.mult, ALU.add, reads=(CONSTb,), writes=(CONSTb,))

            def load_cb(name, src_ap, nblk=8):
                t = smalloc(nblk)
                deferred_par.append((lambda *a_, **k_: (lambda: dma(*a_, **k_)))("pool", t, src_ap.rearrange("(b p) -> p b", p=128), d_par_new(), (), (pbuf(parB_sm),), nonc=True))
                return t

            dwb_t = load_cb("dwb", dw_b)
            lng_t = load_cb("lng", ln_g)
            lnb_t = load_cb("lnb", ln_b)
            dsk_t = load_cb("dsk", d_skip)
            bglu_t = load_cb("bglu", b_glu)
            bgate_t = load_cb("bgate", b_gate, 32)
            dww_t = smalloc(8 * 32).rearrange("p (b k) -> p b k", b=8)

            def pq(ap2):
                return ap2.rearrange("(q gl) p -> (gl p) q", gl=2)

            lre = smalloc(32)
            lim = smalloc(32)
            stp = smalloc(32)
            deferred_par.append((lambda *a_, **k_: (lambda: dma(*a_, **k_)))("pool", lre, pq(lam_re), d_par_new(), (), (pbuf(parB_sm),), nonc=True))
            deferred_par.append((lambda *a_, **k_: (lambda: dma(*a_, **k_)))("pool", lim, pq(lam_im), d_par_new(), (), (pbuf(parB_sm),), nonc=True))
            for gl in range(2):
                src_ap = log_step.rearrange("(q gl) -> gl q", gl=2)[gl:gl + 1, :].broadcast_to([64, 32])
                deferred_par.append((lambda *a_, **k_: (lambda: dma(*a_, **k_)))("pool", stp[gl * 64:(gl + 1) * 64, :], src_ap, d_par_new(), (), (pbuf(parB_sm),), nonc=True))
            TA_f32p = TA[:, :].bitcast(F32)
            bre_t = TA_f32p[:, 1024:1536].rearrange("p (q h) -> p q h", h=16)
            bim_t = TA_f32p[:, 1536:2048].rearrange("p (q h) -> p q h", h=16)
            cn_re = TA_f32p[:, 2048:2560].rearrange("p (b s) -> p b s", b=8)
            cn_im = TA_f32p[:, 2560:3072].rearrange("p (b s) -> p b s", b=8)
            dww_nat = TA_f32p[0:CW, 3072:4096]
            dwwB = Buf()
            deferred_par.append((lambda *a_, **k_: (lambda: dma(*a_, **k_)))("pool", dww_nat, dw_w.rearrange("k o c -> k (o c)"), d_par_new(), (), (dwwB,)))
            deferred_par.append((lambda *a_, **k_: (lambda: dma(*a_, **k_)))("pool", bre_t, b_re.rearrange("(q gl) p h -> (gl p) q h", gl=2), d_par_new(), (), (pbuf(parB_ab),), nonc=True))
            deferred_par.append((lambda *a_, **k_: (lambda: dma(*a_, **k_)))("pool", bim_t, b_im.rearrange("(q gl) p h -> (gl p) q h", gl=2), d_par_new(), (), (pbuf(parB_ab),), nonc=True))
            deferred_par.append((lambda *a_, **k_: (lambda: dma(*a_, **k_)))("pool", cn_re, c_re.rearrange("(b r) h s -> (r h) b s", r=8), d_par_new(), (), (pbuf(parB_ab),), nonc=True))
            deferred_par.append((lambda *a_, **k_: (lambda: dma(*a_, **k_)))("pool", cn_im, c_im.rearrange("(b r) h s -> (r h) b s", r=8), d_par_new(), (), (pbuf(parB_ab),), nonc=True))
            o = 0
            bbr_t = C_f32[:, o:o + 512].rearrange("p (q h) -> p q h", h=16); o += 512
            bbi_t = C_f32[:, o:o + 512].rearrange("p (q h) -> p q h", h=16); o += 512
            tmp3 = C_f32[:, o:o + 512].rearrange("p (q h) -> p q h", h=16); o += 512
            Xre = C_f32[:, o:o + 1024]; o += 1024
            Xim = C_f32[:, o:o + 1024]; o += 1024
            Yre = C_f32[:, o:o + 1024]; o += 1024
            Yim = C_f32[:, o:o + 1024]; o += 1024
            SUb = Buf()

            d_w = [dsrc(f"d_w{i}") for i in range(4)]
            ws_rr = [0]

            def load_w(src_ap, slots, view):
                dma("pool", view, src_ap, d_w[slots[0]], (), tuple(WSb[s] for s in slots))

            def win_slot(c0):
                s = ws_rr[0] % 4
                ws_rr[0] += 1
                v = WS[:, s, :].rearrange("p (k n) -> p k n", k=16)
                load_w(w_in[:, c0:c0 + 256].rearrange("(k p) n -> p k n", p=128), [s], v)
                return s, v

            def proj(wview, wslot, lc, tile, nk, rhs_fn, rhs_bufs, pool=(0, 1, 2, 3, 4, 5, 6, 7)):
                t0, w = tile
                bk = next_bank(pool)
                for k in range(nk):
                    mm(ps[bk][:, 0:w], wview[:, k, lc:lc + 128], rhs_fn(k, t0, w), k == 0, k == nk - 1,
                       reads=(WSb[wslot],) + tuple(rhs_bufs(k, t0)), writes=(psB[bk],), signal=(k == nk - 1))
                return bk

            xn_rhs = lambda k, t0, w: XM[:, k, t0:t0 + w]
            xn_bufs = lambda k, t0: (XMB[k][tidx(t0)],)

            NS1 = 2
            d_x = [dsrc(f"d_x{i}") for i in range(NS1)]
            gfull = C_f32[:, 0:D]
            TA_f32a = TA[:, :].bitcast(F32)
            xts = [C_f32[:, 2048:4096], C_f32[:, 4096:6144]]
            xsb = [Cap[:, 12288:14336], Cap[:, 14336:16384]]
            junk = TA[:, 0:D]
            ssq = smalloc(NS1)
            rst = smalloc(NS1)
            gfB = Buf()
            xtB = [Buf() for _ in range(NS1)]
            xsB = [Buf() for _ in range(NS1)]
            stB = [Buf() for _ in range(NS1)]
            alias_from([gfB] + xtB + xsB, Ab + Cb)
            dma("sp", gfull, norm_g.partition_broadcast(128), d_par_new(), (), (gfB,), nonc=True)
            tiles1 = [(meta, 0, NM)] + [(x[128 * i:128 * (i + 1), :], NM + 128 * i, 128) for i in range(16)]
            uw = [win_slot(C_U + 256 * i) for i in range(4)]
            for fn_ in deferred_par:
                fn_()
            alias_from([SMb], parB_sm)

            def p2a_tile(tile):
                t0, w = tile
                for m in range(8):
                    wslot, wv = uw[m // 2]
                    bk = proj(wv, wslot, (m % 2) * 128, tile, 16, xn_rhs, xn_bufs)
                    cp("act", UBUF[:, m, t0:t0 + w], ps[bk][:, 0:w], reads=(psB[bk],), writes=(Ab[m],))
            for it, (src_ap, tok0, rows) in enumerate(tiles1):
                s = it % NS1
                ti_ = tidx(tok0)
                dma("sp", xts[s][0:rows, :], src_ap, d_x[s], (), (xtB[s],))
                act(junk[0:rows, :], xts[s][0:rows, :], AF.Square, accum=ssq[0:rows, s:s + 1],
                    reads=(xtB[s],), writes=(TAb, stB[s]))
                act(ssq[0:rows, s:s + 1], ssq[0:rows, s:s + 1], AF.Sqrt, bias=NORM_EPS, scale=1.0 / D,
                    reads=(stB[s],), writes=(stB[s],))
                S.op("dve", lambda e, rows=rows, s=s: e.reciprocal(out=rst[0:rows, s:s + 1], in_=ssq[0:rows, s:s + 1]),
                     (stB[s],), (stB[s],))
                stt(xsb[s][0:rows, :], xts[s][0:rows, :], rst[0:rows, s:s + 1], gfull[0:rows, :], ALU.mult, ALU.mult,
                    reads=(xtB[s], stB[s], gfB), writes=(xsB[s],))
                for half in range(2):
                    bk = next_bank()
                    pT = ps[bk][:, :].bitcast(BF16)
                    for kk in range(8):
                        k = half * 8 + kk
                        transpose(pT[:, kk * 128:kk * 128 + rows], xsb[s][0:rows, k * 128:(k + 1) * 128],
                                  ident[0:rows, 0:rows], reads=(xsB[s], CONSTb), writes=(psB[bk],), signal=(kk == 7))
                    src3 = pT.rearrange("p (k t) -> p k t", k=8)[:, :, 0:rows]
                    cp("act" if half == 0 else "dve", XM[:, half * 8:half * 8 + 8, tok0:tok0 + rows], src3,
                       reads=(psB[bk],), writes=tuple(XMB[k_][ti_] for k_ in range(half * 8, half * 8 + 8)))
                if it % 4 == 0 and it >= 4:
                    p2a_tile(TT_ALL[ti_ - 1])
            p2a_tile(TT_ALL[4])
            P1bufs = [gfB] + xtB + xsB

            if dbg is not None and dbg["what"] == "xn":
                dbg_dump(16, lambda m: XM[:, m, :], lambda m: xm_all(m), A_f32[:, 0:L], Ab)

            alias_from(Cb, P1bufs + [TAb])

            if dbg is not None and dbg["what"] == "u":
                dbg_dump(8, lambda m: UBUF[:, m, :], lambda m: (Ab[m],), C_f32[:, 0:L], Cb)

            alias_from([SUb], parB_ab + P1bufs + list(Cb))
            for b in range(8):
                bk = next_bank()
                transpose(ps[bk][:, 0:CW], dww_nat[:, b * 128:(b + 1) * 128], identf[0:CW, 0:CW],
                          reads=(dwwB, CONSTb), writes=(psB[bk],), signal=True)
                cp("dve", dww_t[:, b, 0:CW], ps[bk][:, 0:CW], reads=(psB[bk],), writes=(SMb,))

            def sm32():
                return smalloc(32)

            are, th, mag, om, omf, t1, t2, cth, sth = [sm32() for _ in range(9)]
            lbr, lbi, den, cr, ci = [sm32() for _ in range(5)]
            cw16, sw16, cw512, sw512 = [sm32() for _ in range(4)]
            RW = (SMb,)
            ts("dve", lre, lre, -1e-4, None, ALU.min, reads=RW, writes=RW)
            act(stp, stp, AF.Exp, reads=RW, writes=RW)
            tt("dve", are, lre, stp, ALU.mult, reads=RW, writes=RW)
            tt("dve", th, lim, stp, ALU.mult, reads=RW, writes=RW)
            act(mag, are, AF.Exp, reads=RW, writes=RW)
            ts("dve", om, th, 1.0 / TWO_PI, None, ALU.mult, reads=RW, writes=RW)

            def frac_(dst, src_):
                ts("dve", t1, src_, MAGIC, MAGIC, ALU.add, ALU.subtract, reads=RW, writes=RW)
                tt("dve", dst, src_, t1, ALU.subtract, reads=RW, writes=RW)

            halfpi = smalloc(1)
            S.op("pool", lambda e: e.memset(halfpi, math.pi / 2.0), (), (CONSTb,))

            def cossin_(cdst, sdst, fr, tmp, rw):
                act(sdst, fr, AF.Sin, scale=TWO_PI, reads=rw + (CONSTb,), writes=rw)
                stt(tmp, fr, -1.0, fr, ALU.mult, ALU.max, reads=rw, writes=rw)
                act(cdst, tmp, AF.Sin, bias=halfpi, scale=-TWO_PI, reads=rw + (CONSTb,), writes=rw)

            frac_(omf, om)
            cossin_(cth, sth, omf, t2, RW)
            tt("dve", lbr, mag, cth, ALU.mult, reads=RW, writes=RW)
            tt("dve", lbi, mag, sth, ALU.mult, reads=RW, writes=RW)
            for wdt, cdst, sdst in ((16.0, cw16, sw16), (512.0, cw512, sw512)):
                ts("dve", t2, omf, wdt, None, ALU.mult, reads=RW, writes=RW)
                frac_(om, t2)
                cossin_(cdst, sdst, om, t2, RW)
            nr = om
            ts("dve", nr, lbr, -1.0, None, ALU.add, reads=RW, writes=RW)
            tt("dve", den, lre, lre, ALU.mult, reads=RW, writes=RW)
            tt("dve", t1, lim, lim, ALU.mult, reads=RW, writes=RW)
            tt("dve", den, den, t1, ALU.add, reads=RW, writes=RW)
            S.op("dve", lambda e: e.reciprocal(out=den, in_=den), RW, RW)
            tt("dve", t1, nr, lre, ALU.mult, reads=RW, writes=RW)
            tt("dve", t2, lbi, lim, ALU.mult, reads=RW, writes=RW)
            tt("dve", t1, t1, t2, ALU.add, reads=RW, writes=RW)
            tt("dve", cr, t1, den, ALU.mult, reads=RW, writes=RW)
            tt("dve", t1, lbi, lre, ALU.mult, reads=RW, writes=RW)
            tt("dve", t2, nr, lim, ALU.mult, reads=RW, writes=RW)
            tt("dve", t1, t1, t2, ALU.subtract, reads=RW, writes=RW)
            tt("dve", ci, t1, den, ALU.mult, reads=RW, writes=RW)
            crb = cr.unsqueeze(2).broadcast_to([128, 32, 16])
            cib = ci.unsqueeze(2).broadcast_to([128, 32, 16])
            R2 = (SMb, SUb)
            tt("dve", bbr_t, bre_t, crb, ALU.mult, reads=R2, writes=(SUb,))
            tt("dve", tmp3, bim_t, cib, ALU.mult, reads=R2, writes=(SUb,))
            tt("dve", bbr_t, bbr_t, tmp3, ALU.subtract, reads=R2, writes=(SUb,))
            tt("dve", bbi_t, bim_t, crb, ALU.mult, reads=R2, writes=(SUb,))
            tt("dve", tmp3, bre_t, cib, ALU.mult, reads=R2, writes=(SUb,))
            tt("dve", bbi_t, bbi_t, tmp3, ALU.add, reads=R2, writes=(SUb,))
            for Xt, Bt_ in ((Xre, bbr_t), (Xim, bbi_t)):
                S.op("pool", lambda e, Xt=Xt: e.memset(Xt, 0.0), (), (SUb,))
                X5 = Xt.rearrange("p (b r g h) -> p b r g h", b=8, r=4, g=2)
                B4 = Bt_.rearrange("p (b r) h -> p b r h", r=4)
                for gl in range(2):
                    for b in range(8):
                        cp("dve", X5[gl * 64:(gl + 1) * 64, b, :, gl, :], B4[gl * 64:(gl + 1) * 64, b, :, :],
                           reads=(SUb,), writes=(SUb,))
            for ri, Xt in enumerate((Xre, Xim)):
                for b in range(8):
                    bk = next_bank()
                    transpose(ps[bk][:, 0:128], Xt[:, b * 128:(b + 1) * 128], identf[:, :],
                              reads=(SUb, CONSTb), writes=(psB[bk],), signal=True)
                    cp("act", BT[:, ri, b, :], ps[bk][:, 0:128], reads=(psB[bk],), writes=(S5Wb,))
            for Yt, Cn in ((Yre, cn_re), (Yim, cn_im)):
                Y4 = Yt.rearrange("p (b g s) -> p b g s", b=8, g=2)
                for gl in range(2):
                    ts("dve", Y4[:, :, gl, :], Cn, msk[:, gl:gl + 1], None, ALU.mult,
                       reads=(SUb, CONSTb), writes=(SUb,))
            for ri, Yt in enumerate((Yre, Yim)):
                for b in range(8):
                    bk = next_bank()
                    transpose(ps[bk][:, 0:128], Yt[:, b * 128:(b + 1) * 128], identf[:, :],
                              reads=(SUb, CONSTb), writes=(psB[bk],), signal=True)
                    if ri == 0:
                        cp("act", WC[:, 0, b, :], ps[bk][:, 0:128], reads=(psB[bk],), writes=(S5Wb,))
                        S.op("act", lambda e, b=b, bk=bk: e.mul(out=WC[:, 1, b, :], in_=ps[bk][:, 0:128], mul=-1.0),
                             (psB[bk],), (S5Wb,))
                    else:
                        S.op("act", lambda e, b=b, bk=bk: e.mul(out=WC[:, 2, b, :], in_=ps[bk][:, 0:128], mul=-1.0),
                             (psB[bk],), (S5Wb,))

            for b in range(8):
                act(DSKD[:, b, :], ident[:, :], AF.Copy, scale=dsk_t[:, b:b + 1], reads=(CONSTb, SMb), writes=(S5Wb,))
            CS16, NS16, CS512, NS512 = [smalloc(64).rearrange("p (q c) -> p q c", c=2) for _ in range(4)]
            for CSx, NSx, cwx, swx in ((CS16, NS16, cw16, sw16), (CS512, NS512, cw512, sw512)):
                cp("dve", CSx[:, :, 0], cwx, reads=RW, writes=RW)
                cp("dve", CSx[:, :, 1], swx, reads=RW, writes=RW)
                ts("dve", NSx[:, :, 0], swx, -1.0, None, ALU.mult, reads=RW, writes=RW)
                cp("dve", NSx[:, :, 1], cwx, reads=RW, writes=RW)

            alias_from(Cb, [SUb])

            S5F = []
            for st_ in range(3):
                base = st_ * 2048
                S5F.append(dict(f=[C_f32[:, base + o_ * 512: base + (o_ + 1) * 512] for o_ in (0, 3, 1, 2)],
                                P=C_f32[:, base:base + 1024], Q=C_f32[:, base + 1024:base + 2048],
                                fb=[Buf() for _ in range(4)]))
                alias_from(S5F[-1]["fb"], Cb)
            S5TT = []
            tt3 = smalloc(1024).bitcast(BF16)
            for st_ in range(3):
                if st_ < 2:
                    bfb = A_EL + 2 * (6144 + st_ * 1024)
                    base_ap = AC[:, bfb:bfb + 2048]
                else:
                    base_ap = tt3
                S5TT.append(dict(t=[base_ap[:, o_ * 512:(o_ + 1) * 512] for o_ in (0, 3, 1, 2)],
                                 P=base_ap[:, 0:1024], Q=base_ap[:, 1024:2048],
                                 tb=[Buf() for _ in range(4)]))
                if st_ < 2:
                    alias_from(S5TT[-1]["tb"], Cb)
            gt2 = [WS[:, 0, i * SEQ:(i + 1) * SEQ] for i in range(2)]
            gt2B = [Buf(), Buf()]
            alias_from(gt2B, [WSb[0]])
            TA_f32 = TA[:, :].bitcast(F32)
            TBL = []
            for st_ in range(2):
                base = st_ * 2048
                TBL.append(dict(ec=TA_f32[:, base:base + 512], es=TA_f32[:, base + 512:base + 1024],
                                ta=TA_f32[:, base + 1024:base + 1536], tb=TA_f32[:, base + 1536:base + 2048],
                                b=Buf()))
                alias_from([TBL[-1]["b"]], [TAb, SUb, dwwB, SMb] + P1bufs + parB_ab)
            carry = smalloc(8).rearrange("p (s c) -> p s c", s=2)
            carB = [Buf(), Buf()]
            ctmp = smalloc(4).rearrange("p (s c) -> p s c", s=2)
            GEL = [(SM[:, SMW - 1024:SMW - 512], SM[:, SMW - 512:SMW])] * 2
            GELB = [Buf()] * 2
            YPS = (0, 1)
            PSS = ((2, 3), (4, 5))
            GPS = (6, 7)
            d_gs = [dsrc(f"d_gs{i}") for i in range(2)]

            SL_ZS, SL_G, SL_V, SL_ZC = 32, 40, 48, 56
            scrB = [Buf() for _ in range(64)]
            mstg = smalloc(16).bitcast(BF16).rearrange("p (s c) -> p s c", s=2)
            mstB = [Buf(), Buf()]
            d_ms = [dsrc("d_ms0"), dsrc("d_ms1")]
            jobs = []
            for m in range(8):
                jobs.append(dict(slot=SL_ZS + m, col=C_ZS + 128 * m, func=AF.Silu, bias=None, meta=False))
            for m in range(8):
                jobs.append(dict(slot=SL_G + m, col=C_G + 128 * m, func=AF.Sigmoid, bias=None, meta=True))
            for m in range(8):
                jobs.append(dict(slot=SL_V + m, col=C_V + 128 * m, func=AF.Copy, bias=None, meta=True))
            for m in range(8):
                jobs.append(dict(slot=SL_ZC + m, col=C_ZC + 128 * m, func=AF.Silu, bias=None, meta=False))
            for j in range(32):
                jobs.append(dict(slot=j, col=C_GLC + 128 * j, func=AF.Sigmoid, bias=bgate_t[:, j:j + 1], meta=False))

            def proj_gen():
                nload = len(jobs) // 2

                def gload(li):
                    s_ = 1 + li % 3
                    wv_ = WS[:, s_, :].rearrange("p (k n) -> p k n", k=16)
                    c0 = jobs[2 * li]["col"]
                    load_w(w_in[:, c0:c0 + 256].rearrange("(k p) n -> p k n", p=128), [s_], wv_)
                    return s_, wv_
                nxt = gload(0)
                for ji, jb in enumerate(jobs):
                    if ji % 2 == 0:
                        wslot, wv = nxt
                        if ji // 2 + 1 < nload:
                            nxt = gload(ji // 2 + 1)
                    gsl = ji % 2
                    tiles = TT_ALL if jb["meta"] else TT_REAL
                    for tile in tiles:
                        t0, w = tile
                        bk = proj(wv, wslot, (ji % 2) * 128, tile, 16, xn_rhs, xn_bufs, pool=GPS)
                        if t0 == 0:
                            dst, dB = mstg[:, gsl, :], mstB[gsl]
                        else:
                            dst, dB = gt2[gsl][:, t0 - NM:t0 - NM + w], gt2B[gsl]
                        if jb["bias"] is None:
                            act(dst, ps[bk][:, 0:w], jb["func"], reads=(psB[bk],), writes=(dB,))
                        else:
                            act(dst, ps[bk][:, 0:w], jb["func"], bias=jb["bias"], reads=(psB[bk], SMb), writes=(dB,))
                        yield
                    if jb["meta"]:
                        dma("sp", ps_scr[jb["slot"]][:, 0:NM], mstg[:, gsl, :], d_ms[gsl], (mstB[gsl],),
                            (scrB[jb["slot"]],), nonc=True)
                    dma("sp", ps_scr[jb["slot"]][:, NM:], gt2[gsl], d_gs[gsl], (gt2B[gsl],), (scrB[jb["slot"]],))

            def s5_tables(T, q, part="all"):
                tb = (T["b"],)
                if part in ("all", 1):
                    act(T["ta"], iota_t, AF.Copy, scale=omf[:, q:q + 1], reads=(CONSTb, SMb), writes=tb)
                    act(T["tb"], iota_t, AF.Identity, bias=MAGIC, scale=omf[:, q:q + 1], reads=(CONSTb, SMb), writes=tb)
                    act(T["tb"], T["tb"], AF.Identity, bias=-MAGIC, reads=tb, writes=tb)
                if part in ("all", 2):
                    tt("dve", T["ta"], T["ta"], T["tb"], ALU.subtract, reads=tb, writes=tb)
                    act(T["es"], T["ta"], AF.Sin, scale=TWO_PI, reads=tb, writes=tb)
                    act(T["tb"], T["ta"], AF.Sin, scale=math.pi, reads=tb, writes=tb)
                    act(T["tb"], T["tb"], AF.Square, reads=tb, writes=tb)
                    act(T["ec"], T["tb"], AF.Identity, bias=1.0, scale=-2.0, reads=tb, writes=tb)

            def s5_stageA(sd, part="all"):
                blk, r4, ti, (t0, w), i = sd["blk"], sd["r4"], sd["ti"], sd["tile"], sd["i"]
                T = TBL[sd["pair"] % 2]
                tb = (T["b"],)
                Z = S5F[i % 3]
                f, fb = Z["f"], Z["fb"]
                pr, pi_ = PSS[i % 2]
                rp = 32 * r4
                ub = UBUF[rp:rp + 32, blk, t0:t0 + w]
                if part == "adds":
                    tt(S5ENG, f[0][:, 0:w], f[0][:, 0:w], f[1][:, 0:w], ALU.add, reads=(fb[0], fb[1]), writes=(fb[0],))
                    tt(S5ENG, f[2][:, 0:w], f[2][:, 0:w], f[3][:, 0:w], ALU.subtract, reads=(fb[2], fb[3]),
                       writes=(fb[2],))
                    return
                if part in ("all", "bmm"):
                    mm(ps[pr][:, 0:w], BT[rp:rp + 32, 0, blk, :], ub, True, True,
                       reads=(S5Wb, AbT[blk][ti]), writes=(psB[pr],), signal=True, tile_position=(rp, 0))
                    mm(ps[pi_][:, 0:w], BT[rp:rp + 32, 1, blk, :], ub, True, True,
                       reads=(S5Wb, AbT[blk][ti]), writes=(psB[pi_],), signal=True, tile_position=(rp, 0))
                    if part == "bmm":
                        return
                ec, es_ = T["ec"][:, 0:w], T["es"][:, 0:w]
                psW = ps2[pr // 2][:, :].rearrange("p (b n) -> p b n", b=2)[:, :, 0:w]
                ecb = ec.unsqueeze(1).broadcast_to([128, 2, w])
                esb = es_.unsqueeze(1).broadcast_to([128, 2, w])
                Pv = Z["P"].rearrange("p (b n) -> p b n", b=2)[:, :, 0:w]
                Qv = Z["Q"].rearrange("p (b n) -> p b n", b=2)[:, :, 0:w]
                if part != "modQ":
                    tt("dve", Pv, psW, ecb, ALU.mult, reads=(psB[pr], psB[pi_]) + tb, writes=(fb[0], fb[2]))
                    if part == "modP":
                        return
                tt("dve", Qv, psW, esb, ALU.mult, reads=(psB[pr], psB[pi_]) + tb, writes=(fb[3], fb[1]))
                if part in ("mods", "modQ"):
                    return
                tt(S5ENG, f[0][:, 0:w], f[0][:, 0:w], f[1][:, 0:w], ALU.add, reads=(fb[0], fb[1]), writes=(fb[0],))
                tt(S5ENG, f[2][:, 0:w], f[2][:, 0:w], f[3][:, 0:w], ALU.subtract, reads=(fb[2], fb[3]), writes=(fb[2],))

            def s5_stageB(sd, pending, part="all"):
                blk, r4, ti, (t0, w), i, q = sd["blk"], sd["r4"], sd["ti"], sd["tile"], sd["i"], sd["q"]
                T = TBL[sd["pair"] % 2]
                tb = (T["b"],)
                Z = S5F[i % 3]
                f, fb = Z["f"], Z["fb"]
                tbf, tbb = S5TT[i % 3]["t"], S5TT[i % 3]["tb"]
                rp = 32 * r4
                rq = mag[:, q:q + 1]
                cs = sd["pair"] % 2
                cB = carB[cs]
                ec, es_ = T["ec"][:, 0:w], T["es"][:, 0:w]
                if part in ("all", "scan"):
                    ini_re = 0.0 if ti == 0 else carry[:, cs, 0:1]
                    ini_im = 0.0 if ti == 0 else carry[:, cs, 1:2]
                    crd = () if ti == 0 else (cB,)
                    S.op("dve", lambda e: e.tensor_tensor_scan(
                        out=f[3][:, 0:w], data0=rq.broadcast_to([128, w]), data1=f[0][:, 0:w], initial=ini_re,
                        op0=ALU.mult, op1=ALU.add), (fb[0], SMb) + crd, (fb[3],))
                    S.op("dve", lambda e: e.tensor_tensor_scan(
                        out=f[1][:, 0:w], data0=rq.broadcast_to([128, w]), data1=f[2][:, 0:w], initial=ini_im,
                        op0=ALU.mult, op1=ALU.add), (fb[2], SMb) + crd, (fb[1],))
                if part in ("carryA", "carryB") and ti < len(TT_ALL) - 1:
                    CSq = (CS16 if w == 16 else CS512)[:, q, :]
                    NSq = (NS16 if w == 16 else NS512)[:, q, :]
                    sre_l, sim_l = f[3][:, w - 1:w], f[1][:, w - 1:w]
                    if part == "carryA":
                        tt("dve", ctmp[:, cs, 0:2], sim_l.broadcast_to([128, 2]), NSq, ALU.mult,
                           reads=(fb[1], SMb), writes=(cB,))
                    else:
                        stt(carry[:, cs, 0:2], CSq, sre_l, ctmp[:, cs, 0:2], ALU.mult, ALU.add,
                            reads=(fb[3], SMb, cB), writes=(cB,))
                if part in ("all", "demod") and ti > 0:
                    TT_ = S5TT[i % 3]
                    SQ = Z["Q"].rearrange("p (b n) -> p b n", b=2)[:, :, 0:w]
                    ecb = ec.unsqueeze(1).broadcast_to([128, 2, w])
                    esb = es_.unsqueeze(1).broadcast_to([128, 2, w])
                    tt("dve", TT_["P"].rearrange("p (b n) -> p b n", b=2)[:, :, 0:w], SQ, ecb, ALU.mult,
                       reads=(fb[3], fb[1]) + tb, writes=(tbb[0], tbb[2]))
                    tt("dve", TT_["Q"].rearrange("p (b n) -> p b n", b=2)[:, :, 0:w], SQ, esb, ALU.mult,
                       reads=(fb[3], fb[1]) + tb, writes=(tbb[3], tbb[1]))

                    def cpart(yb=YPS[i % 2]):
                        yo = ps[yb][rp:rp + 32, 0:w]
                        for ci_, wsel in enumerate((0, 1, 2, 2)):
                            mm(yo, WC[:, wsel, blk, rp:rp + 32], tbf[ci_][:, 0:w], ci_ == 0, False,
                               reads=(S5Wb, tbb[ci_]), writes=(psB[yb],), signal=False, tile_position=(0, rp))
                        mm(yo, DSKD[:, blk, rp:rp + 32], UBUF[:, blk, t0:t0 + w], False, True,
                           reads=(S5Wb, AbT[blk][ti]), writes=(psB[yb],), signal=True, tile_position=(0, rp))
                        cp("act", UBUF[rp:rp + 32, blk, t0:t0 + w], ps[yb][rp:rp + 32, 0:w],
                           reads=(psB[yb],), writes=(AbT[blk][ti],))
                    pending.append(cpart)

            def s5_gen():
                steps = []
                pair = 0
                for blk in range(8):
                    for r4 in range(4):
                        for ti, tile in enumerate(TT_ALL):
                            steps.append(dict(blk=blk, r4=r4, q=blk * 4 + r4, ti=ti, tile=tile, pair=pair, i=len(steps)))
                        pair += 1
                n = len(steps)
                pending = []
                epi_q = []
                s5_tables(TBL[0], 0)
                s5_stageA(steps[0])
                s5_stageA(steps[1], "bmm")
                for i in range(n):
                    sd = steps[i]
                    if sd["ti"] == 1 and sd["pair"] + 1 < 32:
                        s5_tables(TBL[(sd["pair"] + 1) % 2], sd["q"] + 1, 1)
                    if sd["ti"] == 2 and sd["pair"] + 1 < 32:
                        s5_tables(TBL[(sd["pair"] + 1) % 2], sd["q"] + 1, 2)
                    prev_pending = pending
                    pending = []
                    s5_stageB(sd, pending, "scan")
                    if i + 1 < n:
                        s5_stageA(steps[i + 1], "modP")
                    s5_stageB(sd, pending, "carryA")
                    if i + 1 < n:
                        s5_stageA(steps[i + 1], "modQ")
                    s5_stageB(sd, pending, "carryB")
                    for fn_ in prev_pending:
                        fn_()
                    if i + 2 < n:
                        s5_stageA(steps[i + 2], "bmm")
                    s5_stageB(sd, pending, "demod")
                    if i + 1 < n:
                        s5_stageA(steps[i + 1], "adds")
                    last_of_block = (sd["r4"] == 3 and sd["ti"] == len(TT_ALL) - 1)
                    if last_of_block:
                        for fn_ in pending:
                            fn_()
                        pending = []
                        blk = sd["blk"]
                        for ti_e, tile_e in enumerate(TT_REAL):
                            def _e1(blk=blk, ti_e=ti_e, tile_e=tile_e):
                                t0, w = tile_e
                                xg = UBUF[:, blk, t0:t0 + w]
                                act(GEL[0][0], xg, AF.Square, reads=(AbT[blk][ti_e + 1],), writes=(GELB[0],))
                                act(GEL[0][0], GEL[0][0], AF.Identity, bias=1.0, scale=0.044715, reads=(GELB[0],),
                                    writes=(GELB[0],))

                            def _e2(blk=blk, ti_e=ti_e, tile_e=tile_e):
                                t0, w = tile_e
                                xg = UBUF[:, blk, t0:t0 + w]
                                tt("dve", GEL[0][1], GEL[0][0], xg, ALU.mult, reads=(AbT[blk][ti_e + 1], GELB[0]),
                                   writes=(GELB[0],))
                                act(GEL[0][1], GEL[0][1], AF.Sigmoid, scale=GELU_C, reads=(GELB[0],), writes=(GELB[0],))

                            def _e3(blk=blk, ti_e=ti_e, tile_e=tile_e):
                                t0, w = tile_e
                                xg = UBUF[:, blk, t0:t0 + w]
                                tt("dve", xg, xg, GEL[0][1], ALU.mult, reads=(AbT[blk][ti_e + 1], GELB[0]),
                                   writes=(AbT[blk][ti_e + 1],))
                            epi_q.extend([_e1, _e2, _e3])
                    elif epi_q:
                        epi_q.pop(0)()
                    yield
                while epi_q:
                    epi_q.pop(0)()

            AbT = [[Buf() for _ in range(5)] for _ in range(8)]
            for b_ in range(8):
                for t_ in range(5):
                    AbT[b_][t_].w = Ab[b_].w
                    AbT[b_][t_].r = list(Ab[b_].r)
            g_s5 = s5_gen()
            g_pj = proj_gen()
            alive_pj = True
            stepn = 0
            for _ in g_s5:
                nun = (1, 2, 2, 2, 2)[stepn % 5]
                stepn += 1
                for _u in range(nun):
                    if alive_pj:
                        try:
                            next(g_pj)
                        except StopIteration:
                            alive_pj = False
            if alive_pj:
                for _ in g_pj:
                    pass
            for b_ in range(8):
                Ab[b_].w = AbT[b_][4].w
                Ab[b_].r = [ev_ for t_ in range(5) for ev_ in ([AbT[b_][t_].w] if AbT[b_][t_].w else []) + AbT[b_][t_].r]

            if dbg is not None and dbg["what"] == "ys0":
                dbg_dump(8, lambda m: UBUF[:, m, :], lambda m: (Ab[m],), TA[:, 0:2 * L].bitcast(F32),
                         [TBL[0]["b"], TBL[1]["b"]])


            alias_from(WSb, gt2B)
            wg_v = WS[:, 0:2, :].rearrange("p s n -> p (s n)").rearrange("p (k n) -> p k n", k=8)
            load_w(w_glu.rearrange("(k p) n -> p k n", p=128), [0, 1], wg_v)
            ws_rr[0] = 2
            alias_from(Cb, [b_ for Z_ in S5F for b_ in Z_["fb"]] + [b_ for Z_ in S5TT for b_ in Z_["tb"]])
            ggt = CBUF[:, :, 0:1024].rearrange("p b (s t) -> p s b t", s=2)
            ggB = [Buf(), Buf()]
            alias_from(ggB, Cb)
            ys_rhs = lambda k, t0, w: UBUF[:, k, t0:t0 + w]
            ys_bufs = lambda k: (Ab[k],)
            for ti, tile in enumerate(TT_REAL):
                t0, w = tile
                gs_ = ti % 2
                for m in range(8):
                    bk = next_bank()
                    for k in range(8):
                        mm(ps[bk][:, 0:w], wg_v[:, k, m * 128:(m + 1) * 128], UBUF[:, k, t0:t0 + w], k == 0, k == 7,
                           reads=(WSb[0], WSb[1], Ab[k]), writes=(psB[bk],), signal=(k == 7))
                    act(ggt[:, gs_, m, 0:w], ps[bk][:, 0:w], AF.Sigmoid, bias=bglu_t[:, m:m + 1],
                        reads=(psB[bk], SMb), writes=(ggB[gs_],))
                tt("dve", UBUF[:, :, t0:t0 + w], UBUF[:, :, t0:t0 + w], ggt[:, gs_, :, 0:w], ALU.mult,
                   reads=tuple(Ab) + (ggB[gs_],), writes=tuple(Ab))

            WSF = WS[:, :, :].rearrange("p s n -> p (s n)")
            RT = [WSF[:, i * L:(i + 1) * L] for i in range(6)]
            RTB = [Buf() for _ in range(6)]
            d_rt = [dsrc(f"d_rt{i}") for i in range(6)]
            rt_rr = [0]
            rt_n = [6]

            def reload(slot, real_only=True):
                i_ = rt_rr[0] % rt_n[0]
                rt_rr[0] += 1
                if real_only:
                    dma("sp", RT[i_][:, NM:], ps_scr[slot][:, NM:], d_rt[i_], (scrB[slot],), (RTB[i_],))
                else:
                    dma("sp", RT[i_], ps_scr[slot], d_rt[i_], (scrB[slot],), (RTB[i_],))
                return RT[i_], RTB[i_]
            alias_from(RTB, WSb)
            zs_t = [reload(SL_ZS + m) for m in range(min(6, 8))]
            for m in range(8):
                zt, zB = zs_t[m] if m < 6 else reload(SL_ZS + m)
                tt("dve", UBUF[:, m, NM:], UBUF[:, m, NM:], zt[:, NM:], ALU.mult, reads=(Ab[m], zB), writes=(Ab[m],))

            if dbg is not None and dbg["what"] == "ys":
                dbg_dump(8, lambda m: UBUF[:, m, :], lambda m: (Ab[m],), C_f32[:, 0:L], list(Cb) + ggB)

            alias_from(Cb, ggB)
            XMF = XM[:, :, :].rearrange("p k t -> p (k t)")
            ABUF2 = XMF[:, 0:A_EL].rearrange("p (b t) -> p b t", b=8)
            X_f32 = XMF[:, A_EL:16 * L].bitcast(F32)
            xm_all_bufs = [b_ for k_ in range(16) for b_ in XMB[k_]]
            A2b = [Buf() for _ in range(8)]
            alias_from(A2b, xm_all_bufs)
            S.op("pool", lambda e: e.memset(ABUF2[:, :, 0:PADL], 0.0), (), tuple(A2b))
            cacc = [X_f32[:, 0:512], X_f32[:, 512:1024]]
            caccB = [Buf(), Buf()]
            alias_from(caccB, xm_all_bufs)
            sgB = [Buf(), Buf()]
            alias_from(sgB, [TBL[0]["b"], TBL[1]["b"], TAb])
            NDT = 8
            NPT = CW - NDT
            DG = [TA[:, 1024 + i * NPT * 128: 1024 + (i + 1) * NPT * 128].rearrange("p (k n) -> p k n", k=NPT)
                  for i in range(2)]
            DGB = [Buf(), Buf()]
            alias_from(DGB, [TBL[0]["b"], TBL[1]["b"], TAb])
            gv_t = {0: (reload(SL_G + 0, False), reload(SL_V + 0, False))}
            for m in range(8):
                if m + 1 < 8:
                    gv_t[m + 1] = (reload(SL_G + m + 1, False), reload(SL_V + m + 1, False))
                (sg_t, sg_B), (v_t, v_B) = gv_t.pop(m)
                tt("dve", ABUF2[:, m, PADL:PADL + L], v_t, sg_t, ALU.mult, reads=(v_B, sg_B), writes=(A2b[m],))
                for k in range(NDT, CW):
                    act(DG[m % 2][:, k - NDT, :], ident[:, :], AF.Copy, scale=dww_t[:, m, k:k + 1],
                        reads=(CONSTb, SMb), writes=(DGB[m % 2],))
                for tp_ in range(0, len(TT_REAL), 2):
                    pair_tiles = TT_REAL[tp_:tp_ + 2]
                    bks = []
                    for (t0, w) in pair_tiles:
                        bk = next_bank()
                        bks.append(bk)
                        for k in range(NDT, CW):
                            mm(ps[bk][:, 0:w], DG[m % 2][:, k - NDT, :], ABUF2[:, m, t0 + k:t0 + k + w], k == NDT,
                               k == CW - 1, reads=(DGB[m % 2], A2b[m]), writes=(psB[bk],), signal=(k == CW - 1))
                    for ai_, (t0, w) in enumerate(pair_tiles):
                        ts("dve", cacc[ai_][:, 0:w], ABUF2[:, m, t0:t0 + w], dww_t[:, m, 0:1], dwb_t[:, m:m + 1],
                           ALU.mult, ALU.add, reads=(A2b[m], SMb), writes=(caccB[ai_],))
                    for k in range(1, NDT):
                        for ai_, (t0, w) in enumerate(pair_tiles):
                            stt(cacc[ai_][:, 0:w], ABUF2[:, m, t0 + k:t0 + k + w], dww_t[:, m, k:k + 1],
                                cacc[ai_][:, 0:w], ALU.mult, ALU.add, reads=(A2b[m], SMb, caccB[ai_]),
                                writes=(caccB[ai_],))
                    for ai_, (t0, w) in enumerate(pair_tiles):
                        tt("dve", CBUF[:, m, t0:t0 + w], ps[bks[ai_]][:, 0:w], cacc[ai_][:, 0:w], ALU.add,
                           reads=(psB[bks[ai_]], caccB[ai_]), writes=(Cb[m],))

            if dbg is not None and dbg["what"] == "a":
                dbg_dump(8, lambda m: ABUF2[:, m, PADL:], lambda m: (A2b[m],), C_f32[:, 0:L], list(Cb))

            if dbg is not None and dbg["what"] == "dg":
                DGf = DG[0].rearrange("p k n -> p (k n)")
                views = [DGf[:, 0:L], DGf[:, 3968 - L:3968], dww_t.rearrange("p b k -> p (b k)")]
                d_dbg = dsrc("d_dbg")
                tmpf = C_f32[:, 0:L]
                evs = []
                for m_ in range(3):
                    wdt_ = views[m_].shape[1]
                    cp("dve", tmpf[:, 0:wdt_], views[m_], reads=(DGB[0], SMb), writes=tuple(Cb))
                    evs.append(dma("sp", dbg_out[m_][:, 0:wdt_], tmpf[:, 0:wdt_], d_dbg, tuple(Cb), ()))
                S.wait_all("sp", evs[-1:])
                dbg_finish(None)

            if dbg is not None and dbg["what"] == "conv":
                dbg_dump(8, lambda m: CBUF[:, m, :], lambda m: (Cb[m],), A_f32[:, 0:L], Ab)

            LNT = [Buf() for _ in range(4)]
            alias_from(LNT, xm_all_bufs + caccB)
            LNb = LNT[0]
            meanT = X_f32[:, 0:L]
            rstdT = X_f32[:, L:2 * L]
            sq_tmp = XMF[:, A_EL + 4 * L: A_EL + 4 * L + 4096].rearrange("p (b t) -> p b t", b=8)
            ln_t = [None, None] + [X_f32[:, 2 * L + 2048 + i * 512: 2 * L + 2048 + (i + 1) * 512] for i in range(2)]
            sqB = Buf()
            alias_from([sqB], xm_all_bufs)
            lnB = [Buf() for _ in range(4)]
            alias_from(lnB, xm_all_bufs)
            for ti, tile in enumerate(TT_REAL):
                t0, w = tile
                act(sq_tmp[:, :, 0:w], CBUF[:, :, t0:t0 + w], AF.Square, reads=tuple(Cb), writes=(sqB,))
                b1 = next_bank()
                for k in range(8):
                    mm(ps[b1][:, 0:w], ones_b[:, :], CBUF[:, k, t0:t0 + w], k == 0, k == 7,
                       reads=(CONSTb, Cb[k]), writes=(psB[b1],), signal=(k == 7))
                b2 = next_bank()
                for k in range(8):
                    mm(ps[b2][:, 0:w], ones_b[:, :], sq_tmp[:, k, 0:w], k == 0, k == 7,
                       reads=(CONSTb, sqB), writes=(psB[b2],), signal=(k == 7))
                mt = meanT[:, t0:t0 + w]
                rt = rstdT[:, t0:t0 + w]
                lt_ = (LNT[ti],)
                S.op("act", lambda e, mt=mt, b1=b1, w=w: e.mul(out=mt, in_=ps[b1][:, 0:w], mul=1.0 / E),
                     (psB[b1],), lt_)
                tt("dve", rt, mt, mt, ALU.mult, reads=lt_, writes=lt_)
                stt(rt, ps[b2][:, 0:w], 1.0 / E, rt, ALU.mult, ALU.subtract, reads=(psB[b2],) + lt_, writes=lt_)
                act(rt, rt, AF.Sqrt, bias=LN_EPS, reads=lt_, writes=lt_)
                S.op("dve", lambda e, rt=rt: e.reciprocal(out=rt, in_=rt), lt_, lt_)

            rt_n[0] = 3
            rt_rr[0] = 0

            ln4 = [X_f32[:, 4128:4640], X_f32[:, 4640:5152], ln_t[2], ln_t[3]]
            alias_from(lnB, [sqB])

            def p2g_gen():
                zc_t = {0: reload(SL_ZC + 0), 1: reload(SL_ZC + 1)}
                deferred = []
                grp = 0
                for m in range(8):
                    for fn_ in deferred:
                        fn_()
                    deferred = []
                    if m + 2 < 8:
                        zc_t[m + 2] = reload(SL_ZC + m + 2)
                    zt, zB = zc_t.pop(m)
                    for tp_ in range(0, 4, 2):
                        tiles_ = TT_REAL[tp_:tp_ + 2]
                        sets_ = [(grp % 2) * 2, (grp % 2) * 2 + 1]
                        grp += 1
                        for (t0, w), si in zip(tiles_, sets_):
                            tt("dve", ln4[si][:, 0:w], CBUF[:, m, t0:t0 + w], meanT[:, t0:t0 + w], ALU.subtract,
                               reads=(Cb[m], LNT[tidx(t0) - 1]), writes=(lnB[si],))
                        for (t0, w), si in zip(tiles_, sets_):
                            tt("dve", ln4[si][:, 0:w], ln4[si][:, 0:w], rstdT[:, t0:t0 + w], ALU.mult,
                               reads=(lnB[si], LNT[tidx(t0) - 1]), writes=(lnB[si],))
                        for (t0, w), si in zip(tiles_, sets_):
                            act(ln4[si][:, 0:w], ln4[si][:, 0:w], AF.Silu, bias=lnb_t[:, m:m + 1], scale=lng_t[:, m:m + 1],
                                reads=(lnB[si], SMb), writes=(lnB[si],))
                        for fn_ in deferred:
                            fn_()
                        deferred = []
                        for (t0, w), si in zip(tiles_, sets_):
                            deferred.append(lambda t0=t0, w=w, si=si, m=m, zt=zt, zB=zB: tt(
                                "dve", CBUF[:, m, t0:t0 + w], ln4[si][:, 0:w], zt[:, t0:t0 + w], ALU.mult,
                                reads=(zB, lnB[si]), writes=(Cb[m],)))
                        yield
                for fn_ in deferred:
                    fn_()

            alias_from([WSb[2]], [RTB[3], RTB[4], RTB[5]])
            alias_from([WSb[3]], [RTB[5]])
            YSB = Ab
            YS = UBUF
            for k_ in range(16):
                alias_from(XMB[k_], A2b if k_ < 8 else (A2b + LNT + [sqB] + lnB + caccB))
            gtl = [TA[:, i * SEQ:(i + 1) * SEQ] for i in range(4)]
            gtB = [Buf() for _ in range(4)]
            alias_from(gtB, sgB + DGB)
            d_gl = [dsrc(f"d_gl{i}") for i in range(4)]
            mtmp = [SM[:, SMW - 1024:SMW - 512], SM[:, SMW - 512:SMW]]
            mtB = [Buf(), Buf()]
            gt_rr = [0]

            def gate_load(slot):
                i_ = gt_rr[0] % 4
                gt_rr[0] += 1
                dma("sp", gtl[i_], ps_scr[slot][:, NM:], d_gl[i_], (scrB[slot],), (gtB[i_],))
                return gtl[i_], gtB[i_]

            def wload(wsrc, c0, slot):
                v_ = WS[:, slot, :].rearrange("p (k n) -> p k n", k=8)
                load_w(wsrc[:, c0:c0 + 512].rearrange("(k p) n -> p k n", p=128), [slot], v_)
                return v_, slot

            def part(j, wv_, wslot_, lc, src, srcB, gate, first):
                g_t, g_B = gate
                for ti, tile in enumerate(TT_REAL):
                    t0, w = tile
                    bk = next_bank()
                    for k in range(8):
                        mm(ps[bk][:, 0:w], wv_[:, k, lc:lc + 128], src[:, k, t0:t0 + w], k == 0, k == 7,
                           reads=(WSb[wslot_], srcB[k]), writes=(psB[bk],), signal=(k == 7))
                    xb = XMB[j][tidx(t0)]
                    if first:
                        tt("dve", XM[:, j, t0:t0 + w], ps[bk][:, 0:w], g_t[:, t0 - NM:t0 - NM + w], ALU.mult,
                           reads=(psB[bk], g_B), writes=(xb,))
                    else:
                        ms = ti % 2
                        tt("dve", mtmp[ms][:, 0:w], ps[bk][:, 0:w], g_t[:, t0 - NM:t0 - NM + w], ALU.mult,
                           reads=(psB[bk], g_B), writes=(mtB[ms],))
                        tt("dve", XM[:, j, t0:t0 + w], XM[:, j, t0:t0 + w], mtmp[ms][:, 0:w], ALU.add,
                           reads=(xb, mtB[ms]), writes=(xb,))
                    yield

            def ssm_lo_gen():
                wA = wload(w_ssm, 0, 2)
                wB = wload(w_ssm, 512, 3)
                for j in range(8):
                    wv_, ws_ = wA if j < 4 else wB
                    yield from part(j, wv_, ws_, (j % 4) * 128, YS, YSB, gate_load(16 + j), True)

            g1_, g2_ = p2g_gen(), ssm_lo_gen()
            a1_, a2_ = True, True
            while a1_ or a2_:
                if a1_:
                    try:
                        next(g1_)
                    except StopIteration:
                        a1_ = False
                for _r in range(2):
                    if a2_:
                        try:
                            next(g2_)
                        except StopIteration:
                            a2_ = False

            if dbg is not None and dbg["what"] == "yc":
                dbg_dump(8, lambda m: CBUF[:, m, :], lambda m: (Cb[m],), A_f32[:, 0:L], list(Ab) + [LNb, sqB] + lnB)

            def drain(g_):
                for _ in g_:
                    pass
            alias_from([WSb[0]], [RTB[0], RTB[1]])
            alias_from([WSb[1]], [RTB[1], RTB[2], RTB[3]])
            wA = wload(w_conv, 0, 0)
            wB = wload(w_conv, 512, 1)
            for j in range(8):
                wv_, ws_ = wA if j < 4 else wB
                drain(part(j, wv_, ws_, (j % 4) * 128, CBUF, Cb, gate_load(j), False))
            for jg in range(2):
                c0 = 1024 + 512 * jg
                wS = wload(w_ssm, c0, 2 if jg == 0 else 0)
                wC = wload(w_conv, c0, 3 if jg == 0 else 1)
                for j in range(8 + 4 * jg, 12 + 4 * jg):
                    lc = (j % 4) * 128
                    drain(part(j, wS[0], wS[1], lc, YS, YSB, gate_load(16 + j), True))
                    drain(part(j, wC[0], wC[1], lc, CBUF, Cb, gate_load(j), False))

            if dbg is not None and dbg["what"] == "merged":
                dbg_dump(16, lambda m: XM[:, m, :], lambda m: xm_all(m), A_f32[:, 0:L], YSB)

            WOb = [Buf() for _ in range(4)]
            alias_from(WOb, YSB + Cb)
            WO = AC[:, 0:16 * D].rearrange("p (k n) -> p k n", k=16)
            d_wo = [dsrc(f"d_wo{i}") for i in range(4)]
            for n in range(4):
                dma("pool", WO[:, :, n * 512:(n + 1) * 512],
                    w_out[:, n * 512:(n + 1) * 512].rearrange("(k p) n -> p k n", p=128), d_wo[n], (), (WOb[n],))
            fgB = Buf()
            alias_from([fgB], gtB)
            fg = TA_f32[:, 0:D]
            dma("sp", fg, final_g.partition_broadcast(128), d_par_new(), (), (fgB,), nonc=True)
            WS_f32 = WS[:, :, :].rearrange("p s n -> p (s n)").bitcast(F32)
            xt4 = [WS_f32[:, 0:D], WS_f32[:, D:2 * D]]
            ht4 = [WS_f32[:, 2 * D:3 * D], WS_f32[:, 3 * D:4 * D]]
            x4B = [Buf(), Buf()]
            h4B = [Buf(), Buf()]
            alias_from(x4B + h4B, WSb)
            junk4 = TA[:, 4096:4096 + D]
            jB = Buf()
            alias_from([jB], gtB)
            st4 = smalloc(4)
            s4B = [Buf(), Buf()]
            d_x4 = [dsrc("d_x4a"), dsrc("d_x4b")]
            d_o = [dsrc("d_o0"), dsrc("d_o1")]
            out_events = []
            dma("sp", xt4[0], x[0:128, :], d_x4[0], (), (x4B[0],))
            for i in range(16):
                s_ = i % 2
                tok0 = NM + 128 * i
                if i + 1 < 16:
                    dma("sp", xt4[1 - s_], x[128 * (i + 1):128 * (i + 2), :], d_x4[1 - s_], (), (x4B[1 - s_],))
                for n in range(4):
                    bk = next_bank()
                    for k in range(16):
                        mm(ps[bk][:, :], XM[:, k, tok0:tok0 + 128], WO[:, k, n * 512:(n + 1) * 512], k == 0, k == 15,
                           reads=(XMB[k][tidx(tok0)], WOb[n]), writes=(psB[bk],), signal=(k == 15))
                    tt("dve", ht4[s_][:, n * 512:(n + 1) * 512], ps[bk][:, :], xt4[s_][:, n * 512:(n + 1) * 512], ALU.add,
                       reads=(psB[bk], x4B[s_]), writes=(h4B[s_],))
                act(junk4, ht4[s_], AF.Square, accum=st4[:, s_:s_ + 1], reads=(h4B[s_],), writes=(jB, s4B[s_]))
                act(st4[:, s_:s_ + 1], st4[:, s_:s_ + 1], AF.Sqrt, bias=NORM_EPS, scale=1.0 / D,
                    reads=(s4B[s_],), writes=(s4B[s_],))
                S.op("dve", lambda e, s_=s_: e.reciprocal(out=st4[:, 2 + s_:3 + s_], in_=st4[:, s_:s_ + 1]),
                     (s4B[s_],), (s4B[s_],))
                stt(ht4[s_], ht4[s_], st4[:, 2 + s_:3 + s_], fg, ALU.mult, ALU.mult,
                    reads=(h4B[s_], s4B[s_], fgB), writes=(h4B[s_],))
                ev = dma("sp", out[128 * i:128 * (i + 1), :], ht4[s_], d_o[s_], (h4B[s_],), ())
                out_events.append(ev)
            S.wait_all("sp", out_events[-2:])

        try:
            program()
        except _Stop:
            pass

        with nc.allow_non_contiguous_dma(reason="small parameter layouts"):
            with nc.Block() as block:
                @block.sync
                def _(e):
                    S.emit("sp", e)

                @block.tensor
                def _(e):
                    S.emit("pe", e)

                @block.scalar
                def _(e):
                    S.emit("act", e)

                @block.vector
                def _(e):
                    S.emit("dve", e)

                @block.gpsimd
                def _(e):
                    S.emit("pool", e)
    return nc


_PARAMS = ["meta", "norm_g", "w_in", "b_gate", "dw_w", "dw_b", "ln_g", "ln_b", "w_conv", "lam_re", "lam_im",
           "log_step", "b_re", "b_im", "c_re", "c_im", "d_skip", "w_glu", "b_glu", "w_ssm", "w_out", "final_g"]


def kernel(**inputs):
    xs = np.ascontiguousarray(np.asarray(inputs["x"], dtype=np.float32))
    params = {k: np.ascontiguousarray(np.asarray(inputs[k], dtype=np.float32)) for k in _PARAMS}
    nc = build()
    in_maps = []
    for c in range(NCORES):
        m = dict(params)
        m["x"] = xs[c]
        in_maps.append(m)
    res = run_bass_kernel_spmd(nc, in_maps, core_ids=list(range(NCORES)))
    return np.stack([np.asarray(r["out"], dtype=np.float32) for r in res.results], axis=0)
```

```python
import math
from contextlib import ExitStack

import numpy as np
import concourse.bass as bass
import concourse.mybir as mybir
from concourse.bass_utils import run_bass_kernel_spmd

F32 = mybir.dt.float32
BF16 = mybir.dt.bfloat16
I32 = mybir.dt.int32
AF = mybir.ActivationFunctionType
ALU = mybir.AluOpType

D = 2048
SEQ = 2048
NM = 16
L = SEQ + NM
NCORES = 8
E = 1024
CW = 31
PADL = CW - 1
NORM_EPS = 1e-6
LN_EPS = 1e-5
SMW = 4352
MAGIC = 12582912.0
TWO_PI = 2.0 * math.pi
GELU_C = 2.0 * math.sqrt(2.0 / math.pi)

TT_ALL = [(0, NM)] + [(NM + 512 * i, 512) for i in range(4)]
TT_REAL = TT_ALL[1:]

C_V, C_G, C_ZC, C_U, C_ZS, C_GLC, C_GLS = 0, 1024, 2048, 3072, 4096, 5120, 7168


SAME_ENGINE_WAITS = ("pool", "act", "dve")
S5ENG = "dve"


class _Stop(Exception):
    pass


class Src:
    def __init__(self, sem, step):
        self.sem, self.step, self.count = sem, step, 0


class Buf:
    __slots__ = ("w", "r")

    def __init__(self):
        self.w = None
        self.r = []


def alias_from(new_bufs, old_bufs):
    evs = []
    for b in old_bufs:
        if b.w is not None:
            evs.append(b.w)
        evs.extend(b.r)
    for nb in new_bufs:
        nb.r = list(nb.r) + evs


class Sched:
    def __init__(self):
        self.eng = {}

    def add_engine(self, name, sem):
        self.eng[name] = dict(src=Src(sem, 1), insts=[], waited={})

    def op(self, eng, fn, reads=(), writes=(), signal=True, src=None):
        En = self.eng[eng]
        s = src if src is not None else En["src"]
        need = {}
        srcs = {}
        deps = []
        for b in reads:
            if b.w is not None:
                deps.append(b.w)
        for b in writes:
            if b.w is not None:
                deps.append(b.w)
            deps.extend(b.r)
        for (ds, v) in deps:
            if ds is En["src"] and (eng == "pe" or eng not in SAME_ENGINE_WAITS):
                continue
            k = id(ds)
            if v > need.get(k, 0):
                need[k] = v
                srcs[k] = ds
        waits = []
        for k, v in need.items():
            if En["waited"].get(k, 0) >= v:
                continue
            En["waited"][k] = v
            waits.append((srcs[k].sem, v))
        if signal:
            s.count += s.step
            ev = (s, s.count)
            sig = (s.sem, s.step)
        else:
            ev = (s, s.count + s.step)
            sig = None
        En["insts"].append((waits, fn, sig))
        for b in reads:
            b.r.append(ev)
        for b in writes:
            b.w = ev
            b.r = []
        return ev

    def wait_all(self, eng, events):
        En = self.eng[eng]
        waits = []
        for (ds, v) in events:
            k = id(ds)
            if En["waited"].get(k, 0) >= v:
                continue
            En["waited"][k] = v
            waits.append((ds.sem, v))
        En["insts"].append((waits, None, None))

    def emit(self, name, e):
        for waits, fn, sig in self.eng[name]["insts"]:
            for sem, v in waits:
                e.wait_ge(sem, v)
            if fn is None:
                continue
            ins = fn(e)
            if sig is not None:
                ins.then_inc(sig[0], sig[1])


def build(dbg=None):
    nc = bass.Bass("TRN2", target_bir_lowering=False)
    S = Sched()

    def din(name, shape):
        return nc.dram_tensor(name, list(shape), F32, kind="ExternalInput").ap()

    x = din("x", [SEQ, D])
    meta = din("meta", [NM, D])
    norm_g = din("norm_g", [D])
    w_in = din("w_in", [D, 9216])
    b_gate = din("b_gate", [4096])
    dw_w = din("dw_w", [CW, 1, E])
    dw_b = din("dw_b", [E])
    ln_g = din("ln_g", [E])
    ln_b = din("ln_b", [E])
    w_conv = din("w_conv", [E, D])
    lam_re = din("lam_re", [64, 64])
    lam_im = din("lam_im", [64, 64])
    log_step = din("log_step", [64])
    b_re = din("b_re", [64, 64, 16])
    b_im = din("b_im", [64, 64, 16])
    c_re = din("c_re", [64, 16, 64])
    c_im = din("c_im", [64, 16, 64])
    d_skip = din("d_skip", [E])
    w_glu = din("w_glu", [E, E])
    b_glu = din("b_glu", [E])
    w_ssm = din("w_ssm", [E, D])
    w_out = din("w_out", [D, D])
    final_g = din("final_g", [D])
    out = nc.dram_tensor("out", [SEQ, D], F32, kind="ExternalOutput").ap()
    ps_scr = nc.dram_tensor("ps_scr", [64, 128, L], BF16).ap()
    dbg_out = None
    if dbg is not None:
        dbg_out = nc.dram_tensor("dbg", list(dbg["shape"]), F32, kind="ExternalOutput").ap()

    es = ExitStack()
    with es:
        def sb(name, shape, dt):
            return es.enter_context(nc.sbuf_tensor(name, list(shape), dt))

        def sem(name):
            return es.enter_context(nc.semaphore(name))

        XM = sb("XM", [128, 16, L], BF16)
        AC = sb("AC", [128, 33264], BF16)
        WS = sb("WS", [128, 4, 4096], BF16)
        TA = sb("TA", [128, 8192], BF16)
        SM = sb("SM", [128, SMW], F32)
        ident = sb("ident", [128, 128], BF16)
        identf = sb("identf", [128, 128], F32)
        ones_b = sb("ones_b", [128, 128], BF16)
        BT = sb("BT", [128, 2, 8, 128], BF16)
        WC = sb("WC", [128, 3, 8, 128], BF16)
        DSKD = sb("DSKD", [128, 8, 128], BF16)
        ps2 = [es.enter_context(nc.psum_tensor(f"ps{i}", [128, 1024], F32)) for i in range(4)]
        ps = [ps2[j // 2][:, (j % 2) * 512:(j % 2 + 1) * 512] for j in range(8)]
        psB = [Buf() for _ in range(8)]

        A_EL = 8 * (L + PADL)
        Aap = AC[:, 0:A_EL]
        Cap = AC[:, A_EL:A_EL + 8 * L]
        ABUF = Aap.rearrange("p (b t) -> p b t", b=8)
        UBUF = ABUF[:, :, PADL:]
        CBUF = Cap.rearrange("p (b t) -> p b t", b=8)
        A_f32 = Aap.bitcast(F32)
        C_f32 = Cap.bitcast(F32)
        AC_f32 = AC[:, :].bitcast(F32)

        for n in ("pe", "act", "dve", "pool", "sp"):
            S.add_engine(n, sem("s_" + n))

        def dsrc(name):
            return Src(sem(name), 16)

        XMB = [[Buf() for _ in range(5)] for _ in range(16)]

        def tidx(t0):
            return 0 if t0 < NM else 1 + (t0 - NM) // 512

        def xm_all(k):
            return tuple(XMB[k])
        WSb = [Buf() for _ in range(4)]
        Ab = [Buf() for _ in range(8)]
        Cb = [Buf() for _ in range(8)]
        TAb = Buf()
        SMb = Buf()
        CONSTb = Buf()
        S5Wb = Buf()

        sm_off = [0]

        def smalloc(n):
            o = sm_off[0]
            sm_off[0] += n
            assert sm_off[0] <= SMW - 1024, sm_off[0]
            return SM[:, o:o + n]

        def act(out_, in_, func, bias=0.0, scale=1.0, accum=None, reads=(), writes=()):
            if accum is None:
                return S.op("act", lambda e: e.activation(out=out_, in_=in_, func=func, bias=bias, scale=scale),
                            reads, writes)
            return S.op("act", lambda e: e.activation(out=out_, in_=in_, func=func, bias=bias, scale=scale,
                                                      accum_out=accum), reads, writes)

        def tt(eng, out_, in0, in1, op, reads=(), writes=()):
            return S.op(eng, lambda e: e.tensor_tensor(out=out_, in0=in0, in1=in1, op=op), reads, writes)

        def ts(eng, out_, in0, s1, s2, op0, op1=None, reads=(), writes=()):
            if op1 is None:
                return S.op(eng, lambda e: e.tensor_scalar(out=out_, in0=in0, scalar1=s1, scalar2=None, op0=op0),
                            reads, writes)
            return S.op(eng, lambda e: e.tensor_scalar(out=out_, in0=in0, scalar1=s1, scalar2=s2, op0=op0, op1=op1),
                        reads, writes)

        def stt(out_, in0, scalar, in1, op0, op1, reads=(), writes=()):
            return S.op("dve", lambda e: e.scalar_tensor_tensor(out=out_, in0=in0, scalar=scalar, in1=in1,
                                                                op0=op0, op1=op1), reads, writes)

        def cp(eng, out_, in_, reads=(), writes=()):
            if eng == "act":
                return S.op("act", lambda e: e.copy(out=out_, in_=in_), reads, writes)
            return S.op(eng, lambda e: e.tensor_copy(out=out_, in_=in_), reads, writes)

        def mm(out_, lhsT, rhs, start, stop, reads=(), writes=(), signal=False, tile_position=None):
            if tile_position is None:
                return S.op("pe", lambda e: e.matmul(out_, lhsT, rhs, start=start, stop=stop),
                            reads, writes, signal=signal)
            return S.op("pe", lambda e: e.matmul(out_, lhsT, rhs, start=start, stop=stop,
                                                 tile_position=tile_position), reads, writes, signal=signal)

        def transpose(out_, in_, idn, reads=(), writes=(), signal=False):
            return S.op("pe", lambda e: e.transpose(out_, in_, idn), reads, writes, signal=signal)

        def dma(q, out_, in_, src, reads=(), writes=(), nonc=False):
            if nonc:
                return S.op(q, lambda e: e.dma_start(out=out_, in_=in_, allow_slow_non_contiguous=True),
                            reads, writes, src=src)
            return S.op(q, lambda e: e.dma_start(out=out_, in_=in_), reads, writes, src=src)

        bank_rr = [0]

        def next_bank(pool=(0, 1, 2, 3, 4, 5, 6, 7)):
            b = pool[bank_rr[0] % len(pool)]
            bank_rr[0] += 1
            return b

        def dbg_finish(bufs):
            d_dbg2 = dsrc("d_dbgF")
            dma("sp", out[0:128, 0:128], identf[:, :], d_dbg2, (CONSTb,), ())
            S.wait_all("sp", [(d_dbg2, d_dbg2.count)])
            raise _Stop()

        def dbg_dump(nblk, view_fn, bufs_fn, tmpf, tmp_bufs):
            evs = []
            for m in range(nblk):
                cp("dve", tmpf, view_fn(m), reads=tuple(bufs_fn(m)), writes=tuple(tmp_bufs))
                evs.append(dma("sp", dbg_out[m], tmpf, dsrc(f"d_dbg{m}"), tuple(tmp_bufs), ()))
            S.wait_all("sp", evs[-1:])
            dbg_finish(None)

        def program():
            par_n = [0]
            deferred_par = []
            parB_sm = []
            parB_ab = []

            def pbuf(lst):
                b_ = Buf()
                lst.append(b_)
                return b_

            def d_par_new():
                par_n[0] += 1
                return dsrc(f"d_par{par_n[0]}")
            io_jp = smalloc(128)
            S.op("pool", lambda e: e.iota(io_jp, [[1, 128]], base=0, channel_multiplier=-1,
                                          allow_small_or_imprecise_dtypes=True), (), (CONSTb,))
            ts("dve", identf[:, :], io_jp, 0.0, None, ALU.is_equal, reads=(CONSTb,), writes=(CONSTb,))
            cp("dve", ident[:, :], identf[:, :], reads=(CONSTb,), writes=(CONSTb,))
            S.op("pool", lambda e: e.memset(ones_b[:, :], 1.0), (), (CONSTb,))
            iota_t = smalloc(512)
            S.op("pool", lambda e: e.iota(iota_t, [[1, 512]], base=0, channel_multiplier=0,
                                          allow_small_or_imprecise_dtypes=True), (), (CONSTb,))
            mski = smalloc(2).bitcast(I32)
            msk = smalloc(2)
            S.op("pool", lambda e: e.iota(mski[:, 0:1], [[0, 1]], base=0, channel_multiplier=1), (), (CONSTb,))
            ts("dve", mski[:, 0:1], mski[:, 0:1], 4, 1, ALU.arith_shift_right, ALU.bitwise_and,
               reads=(CONSTb,), writes=(CONSTb,))
            cp("dve", msk[:, 1:2], mski[:, 0:1], reads=(CONSTb,), writes=(CONSTb,))
            ts("dve", msk[:, 0:1], msk[:, 1:2], -1.0, 1.0, ALU.mult, ALU.add, reads=(CONSTb,), writes=(CONSTb,))

            def load_cb(name, src_ap, nblk=8):
                t = smalloc(nblk)
                deferred_par.append((lambda *a_, **k_: (lambda: dma(*a_, **k_)))("pool", t, src_ap.rearrange("(b p) -> p b", p=128), d_par_new(), (), (pbuf(parB_sm),), nonc=True))
                return t

            dwb_t = load_cb("dwb", dw_b)
            lng_t = load_cb("lng", ln_g)
            lnb_t = load_cb("lnb", ln_b)
            dsk_t = load_cb("dsk", d_skip)
            bglu_t = load_cb("bglu", b_glu)
            bgate_t = load_cb("bgate", b_gate, 32)
            dww_t = smalloc(8 * 32).rearrange("p (b k) -> p b k", b=8)

            def pq(ap2):
                return ap2.rearrange("(q gl) p -> (gl p) q", gl=2)

            lre = smalloc(32)
            lim = smalloc(32)
            stp = smalloc(32)
            deferred_par.append((lambda *a_, **k_: (lambda: dma(*a_, **k_)))("pool", lre, pq(lam_re), d_par_new(), (), (pbuf(parB_sm),), nonc=True))
            deferred_par.append((lambda *a_, **k_: (lambda: dma(*a_, **k_)))("pool", lim, pq(lam_im), d_par_new(), (), (pbuf(parB_sm),), nonc=True))
            for gl in range(2):
                src_ap = log_step.rearrange("(q gl) -> gl q", gl=2)[gl:gl + 1, :].broadcast_to([64, 32])
                deferred_par.append((lambda *a_, **k_: (lambda: dma(*a_, **k_)))("pool", stp[gl * 64:(gl + 1) * 64, :], src_ap, d_par_new(), (), (pbuf(parB_sm),), nonc=True))
            TA_f32p = TA[:, :].bitcast(F32)
            bre_t = TA_f32p[:, 1024:1536].rearrange("p (q h) -> p q h", h=16)
            bim_t = TA_f32p[:, 1536:2048].rearrange("p (q h) -> p q h", h=16)
            cn_re = TA_f32p[:, 2048:2560].rearrange("p (b s) -> p b s", b=8)
            cn_im = TA_f32p[:, 2560:3072].rearrange("p (b s) -> p b s", b=8)
            dww_nat = TA_f32p[0:CW, 3072:4096]
            dwwB = Buf()
            deferred_par.append((lambda *a_, **k_: (lambda: dma(*a_, **k_)))("pool", dww_nat, dw_w.rearrange("k o c -> k (o c)"), d_par_new(), (), (dwwB,)))
            deferred_par.append((lambda *a_, **k_: (lambda: dma(*a_, **k_)))("pool", bre_t, b_re.rearrange("(q gl) p h -> (gl p) q h", gl=2), d_par_new(), (), (pbuf(parB_ab),), nonc=True))
            deferred_par.append((lambda *a_, **k_: (lambda: dma(*a_, **k_)))("pool", bim_t, b_im.rearrange("(q gl) p h -> (gl p) q h", gl=2), d_par_new(), (), (pbuf(parB_ab),), nonc=True))
            deferred_par.append((lambda *a_, **k_: (lambda: dma(*a_, **k_)))("pool", cn_re, c_re.rearrange("(b r) h s -> (r h) b s", r=8), d_par_new(), (), (pbuf(parB_ab),), nonc=True))
            deferred_par.append((lambda *a_, **k_: (lambda: dma(*a_, **k_)))("pool", cn_im, c_im.rearrange("(b r) h s -> (r h) b s", r=8), d_par_new(), (), (pbuf(parB_ab),), nonc=True))
            o = 0
            bbr_t = C_f32[:, o:o + 512].rearrange("p (q h) -> p q h", h=16); o += 512
            bbi_t = C_f32[:, o:o + 512].rearrange("p (q h) -> p q h", h=16); o += 512
            tmp3 = C_f32[:, o:o + 512].rearrange("p (q h) -> p q h", h=16); o += 512
            Xre = C_f32[:, o:o + 1024]; o += 1024
            Xim = C_f32[:, o:o + 1024]; o += 1024
            Yre = C_f32[:, o:o + 1024]; o += 1024
            Yim = C_f32[:, o:o + 1024]; o += 1024
            SUb = Buf()

            d_w = [dsrc(f"d_w{i}") for i in range(4)]
            ws_rr = [0]

            def load_w(src_ap, slots, view):
                dma("pool", view, src_ap, d_w[slots[0]], (), tuple(WSb[s] for s in slots))

            def win_slot(c0):
                s = ws_rr[0] % 4
                ws_rr[0] += 1
                v = WS[:, s, :].rearrange("p (k n) -> p k n", k=16)
                load_w(w_in[:, c0:c0 + 256].rearrange("(k p) n -> p k n", p=128), [s], v)
                return s, v

            def proj(wview, wslot, lc, tile, nk, rhs_fn, rhs_bufs, pool=(0, 1, 2, 3, 4, 5, 6, 7)):
                t0, w = tile
                bk = next_bank(pool)
                for k in range(nk):
                    mm(ps[bk][:, 0:w], wview[:, k, lc:lc + 128], rhs_fn(k, t0, w), k == 0, k == nk - 1,
                       reads=(WSb[wslot],) + tuple(rhs_bufs(k, t0)), writes=(psB[bk],), signal=(k == nk - 1))
                return bk

            xn_rhs = lambda k, t0, w: XM[:, k, t0:t0 + w]
            xn_bufs = lambda k, t0: (XMB[k][tidx(t0)],)

            NS1 = 2
            d_x = [dsrc(f"d_x{i}") for i in range(NS1)]
            gfull = C_f32[:, 0:D]
            TA_f32a = TA[:, :].bitcast(F32)
            xts = [C_f32[:, 2048:4096], C_f32[:, 4096:6144]]
            xsb = [Cap[:, 12288:14336], Cap[:, 14336:16384]]
            junk = TA[:, 0:D]
            ssq = smalloc(NS1)
            rst = smalloc(NS1)
            gfB = Buf()
            xtB = [Buf() for _ in range(NS1)]
            xsB = [Buf() for _ in range(NS1)]
            stB = [Buf() for _ in range(NS1)]
            alias_from([gfB] + xtB + xsB, Ab + Cb)
            dma("sp", gfull, norm_g.partition_broadcast(128), d_par_new(), (), (gfB,), nonc=True)
            tiles1 = [(meta, 0, NM)] + [(x[128 * i:128 * (i + 1), :], NM + 128 * i, 128) for i in range(16)]
            uw = [win_slot(C_U + 256 * i) for i in range(4)]
            for fn_ in deferred_par:
                fn_()
            alias_from([SMb], parB_sm)

            def p2a_tile(tile):
                t0, w = tile
                for m in range(8):
                    wslot, wv = uw[m // 2]
                    bk = proj(wv, wslot, (m % 2) * 128, tile, 16, xn_rhs, xn_bufs)
                    cp("act", UBUF[:, m, t0:t0 + w], ps[bk][:, 0:w], reads=(psB[bk],), writes=(Ab[m],))
            for it, (src_ap, tok0, rows) in enumerate(tiles1):
                s = it % NS1
                ti_ = tidx(tok0)
                dma("sp", xts[s][0:rows, :], src_ap, d_x[s], (), (xtB[s],))
                act(junk[0:rows, :], xts[s][0:rows, :], AF.Square, accum=ssq[0:rows, s:s + 1],
                    reads=(xtB[s],), writes=(TAb, stB[s]))
                act(ssq[0:rows, s:s + 1], ssq[0:rows, s:s + 1], AF.Sqrt, bias=NORM_EPS, scale=1.0 / D,
                    reads=(stB[s],), writes=(stB[s],))
                S.op("dve", lambda e, rows=rows, s=s: e.reciprocal(out=rst[0:rows, s:s + 1], in_=ssq[0:rows, s:s + 1]),
                     (stB[s],), (stB[s],))
                stt(xsb[s][0:rows, :], xts[s][0:rows, :], rst[0:rows, s:s + 1], gfull[0:rows, :], ALU.mult, ALU.mult,
                    reads=(xtB[s], stB[s], gfB), writes=(xsB[s],))
                for half in range(2):
                    bk = next_bank()
                    pT = ps[bk][:, :].bitcast(BF16)
                    for kk in range(8):
                        k = half * 8 + kk
                        transpose(pT[:, kk * 128:kk * 128 + rows], xsb[s][0:rows, k * 128:(k + 1) * 128],
                                  ident[0:rows, 0:rows], reads=(xsB[s], CONSTb), writes=(psB[bk],), signal=(kk == 7))
                    src3 = pT.rearrange("p (k t) -> p k t", k=8)[:, :, 0:rows]
                    cp("act" if half == 0 else "dve", XM[:, half * 8:half * 8 + 8, tok0:tok0 + rows], src3,
                       reads=(psB[bk],), writes=tuple(XMB[k_][ti_] for k_ in range(half * 8, half * 8 + 8)))
                if it % 4 == 0 and it >= 4:
                    p2a_tile(TT_ALL[ti_ - 1])
            p2a_tile(TT_ALL[4])
            P1bufs = [gfB] + xtB + xsB

            if dbg is not None and dbg["what"] == "xn":
                dbg_dump(16, lambda m: XM[:, m, :], lambda m: xm_all(m), A_f32[:, 0:L], Ab)

            alias_from(Cb, P1bufs + [TAb])

            if dbg is not None and dbg["what"] == "u":
                dbg_dump(8, lambda m: UBUF[:, m, :], lambda m: (Ab[m],), C_f32[:, 0:L], Cb)

            alias_from([SUb], parB_ab + P1bufs + list(Cb))
            for b in range(8):
                bk = next_bank()
                transpose(ps[bk][:, 0:CW], dww_nat[:, b * 128:(b + 1) * 128], identf[0:CW, 0:CW],
                          reads=(dwwB, CONSTb), writes=(psB[bk],), signal=True)
                cp("dve", dww_t[:, b, 0:CW], ps[bk][:, 0:CW], reads=(psB[bk],), writes=(SMb,))

            def sm32():
                return smalloc(32)

            are, th, mag, om, omf, t1, t2, cth, sth = [sm32() for _ in range(9)]
            lbr, lbi, den, cr, ci = [sm32() for _ in range(5)]
            cw16, sw16, cw512, sw512 = [sm32() for _ in range(4)]
            RW = (SMb,)
            ts("dve", lre, lre, -1e-4, None, ALU.min, reads=RW, writes=RW)
            act(stp, stp, AF.Exp, reads=RW, writes=RW)
            tt("dve", are, lre, stp, ALU.mult, reads=RW, writes=RW)
            tt("dve", th, lim, stp, ALU.mult, reads=RW, writes=RW)
            act(mag, are, AF.Exp, reads=RW, writes=RW)
            ts("dve", om, th, 1.0 / TWO_PI, None, ALU.mult, reads=RW, writes=RW)

            def frac_(dst, src_):
                ts("dve", t1, src_, MAGIC, MAGIC, ALU.add, ALU.subtract, reads=RW, writes=RW)
                tt("dve", dst, src_, t1, ALU.subtract, reads=RW, writes=RW)

            halfpi = smalloc(1)
            S.op("pool", lambda e: e.memset(halfpi, math.pi / 2.0), (), (CONSTb,))

            def cossin_(cdst, sdst, fr, tmp, rw):
                act(sdst, fr, AF.Sin, scale=TWO_PI, reads=rw + (CONSTb,), writes=rw)
                stt(tmp, fr, -1.0, fr, ALU.mult, ALU.max, reads=rw, writes=rw)
                act(cdst, tmp, AF.Sin, bias=halfpi, scale=-TWO_PI, reads=rw + (CONSTb,), writes=rw)

            frac_(omf, om)
            cossin_(cth, sth, omf, t2, RW)
            tt("dve", lbr, mag, cth, ALU.mult, reads=RW, writes=RW)
            tt("dve", lbi, mag, sth, ALU.mult, reads=RW, writes=RW)
            for wdt, cdst, sdst in ((16.0, cw16, sw16), (512.0, cw512, sw512)):
                ts("dve", t2, omf, wdt, None, ALU.mult, reads=RW, writes=RW)
                frac_(om, t2)
                cossin_(cdst, sdst, om, t2, RW)
            nr = om
            ts("dve", nr, lbr, -1.0, None, ALU.add, reads=RW, writes=RW)
            tt("dve", den, lre, lre, ALU.mult, reads=RW, writes=RW)
            tt("dve", t1, lim, lim, ALU.mult, reads=RW, writes=RW)
            tt("dve", den, den, t1, ALU.add, reads=RW, writes=RW)
            S.op("dve", lambda e: e.reciprocal(out=den, in_=den), RW, RW)
            tt("dve", t1, nr, lre, ALU.mult, reads=RW, writes=RW)
            tt("dve", t2, lbi, lim, ALU.mult, reads=RW, writes=RW)
            tt("dve", t1, t1, t2, ALU.add, reads=RW, writes=RW)
            tt("dve", cr, t1, den, ALU.mult, reads=RW, writes=RW)
            tt("dve", t1, lbi, lre, ALU.mult, reads=RW, writes=RW)
            tt("dve", t2, nr, lim, ALU.mult, reads=RW, writes=RW)
            tt("dve", t1, t1, t2, ALU.subtract, reads=RW, writes=RW)
            tt("dve", ci, t1, den, ALU.mult, reads=RW, writes=RW)
            crb = cr.unsqueeze(2).broadcast_to([128, 32, 16])
            cib = ci.unsqueeze(2).broadcast_to([128, 32, 16])
            R2 = (SMb, SUb)
            tt("dve", bbr_t, bre_t, crb, ALU.mult, reads=R2, writes=(SUb,))
            tt("dve", tmp3, bim_t, cib, ALU.mult, reads=R2, writes=(SUb,))
            tt("dve", bbr_t, bbr_t, tmp3, ALU.subtract, reads=R2, writes=(SUb,))
            tt("dve", bbi_t, bim_t, crb, ALU.mult, reads=R2, writes=(SUb,))
            tt("dve", tmp3, bre_t, cib, ALU.mult, reads=R2, writes=(SUb,))
            tt("dve", bbi_t, bbi_t, tmp3, ALU.add, reads=R2, writes=(SUb,))
            for Xt, Bt_ in ((Xre, bbr_t), (Xim, bbi_t)):
                S.op("pool", lambda e, Xt=Xt: e.memset(Xt, 0.0), (), (SUb,))
                X5 = Xt.rearrange("p (b r g h) -> p b r g h", b=8, r=4, g=2)
                B4 = Bt_.rearrange("p (b r) h -> p b r h", r=4)
                for gl in range(2):
                    for b in range(8):
                        cp("dve", X5[gl * 64:(gl + 1) * 64, b, :, gl, :], B4[gl * 64:(gl + 1) * 64, b, :, :],
                           reads=(SUb,), writes=(SUb,))
            for ri, Xt in enumerate((Xre, Xim)):
                for b in range(8):
                    bk = next_bank()
                    transpose(ps[bk][:, 0:128], Xt[:, b * 128:(b + 1) * 128], identf[:, :],
                              reads=(SUb, CONSTb), writes=(psB[bk],), signal=True)
                    cp("act", BT[:, ri, b, :], ps[bk][:, 0:128], reads=(psB[bk],), writes=(S5Wb,))
            for Yt, Cn in ((Yre, cn_re), (Yim, cn_im)):
                Y4 = Yt.rearrange("p (b g s) -> p b g s", b=8, g=2)
                for gl in range(2):
                    ts("dve", Y4[:, :, gl, :], Cn, msk[:, gl:gl + 1], None, ALU.mult,
                       reads=(SUb, CONSTb), writes=(SUb,))
            for ri, Yt in enumerate((Yre, Yim)):
                for b in range(8):
                    bk = next_bank()
                    transpose(ps[bk][:, 0:128], Yt[:, b * 128:(b + 1) * 128], identf[:, :],
                              reads=(SUb, CONSTb), writes=(psB[bk],), signal=True)
                    if ri == 0:
                        cp("act", WC[:, 0, b, :], ps[bk][:, 0:128], reads=(psB[bk],), writes=(S5Wb,))
                        S.op("act", lambda e, b=b, bk=bk: e.mul(out=WC[:, 1, b, :], in_=ps[bk][:, 0:128], mul=-1.0),
                             (psB[bk],), (S5Wb,))
                    else:
                        S.op("act", lambda e, b=b, bk=bk: e.mul(out=WC[:, 2, b, :], in_=ps[bk][:, 0:128], mul=-1.0),
                             (psB[bk],), (S5Wb,))

            for b in range(8):
                act(DSKD[:, b, :], ident[:, :], AF.Copy, scale=dsk_t[:, b:b + 1], reads=(CONSTb, SMb), writes=(S5Wb,))
            CS16, NS16, CS512, NS512 = [smalloc(64).rearrange("p (q c) -> p q c", c=2) for _ in range(4)]
            for CSx, NSx, cwx, swx in ((CS16, NS16, cw16, sw16), (CS512, NS512, cw512, sw512)):
                cp("dve", CSx[:, :, 0], cwx, reads=RW, writes=RW)
                cp("dve", CSx[:, :, 1], swx, reads=RW, writes=RW)
                ts("dve", NSx[:, :, 0], swx, -1.0, None, ALU.mult, reads=RW, writes=RW)
                cp("dve", NSx[:, :, 1], cwx, reads=RW, writes=RW)

            alias_from(Cb, [SUb])

            S5F = []
            for st_ in range(3):
                base = st_ * 2048
                S5F.append(dict(f=[C_f32[:, base + o_ * 512: base + (o_ + 1) * 512] for o_ in (0, 3, 1, 2)],
                                P=C_f32[:, base:base + 1024], Q=C_f32[:, base + 1024:base + 2048],
                                fb=[Buf() for _ in range(4)]))
                alias_from(S5F[-1]["fb"], Cb)
            S5TT = []
            tt3 = smalloc(1024).bitcast(BF16)
            for st_ in range(3):
                if st_ < 2:
                    bfb = A_EL + 2 * (6144 + st_ * 1024)
                    base_ap = AC[:, bfb:bfb + 2048]
                else:
                    base_ap = tt3
                S5TT.append(dict(t=[base_ap[:, o_ * 512:(o_ + 1) * 512] for o_ in (0, 3, 1, 2)],
                                 P=base_ap[:, 0:1024], Q=base_ap[:, 1024:2048],
                                 tb=[Buf() for _ in range(4)]))
                if st_ < 2:
                    alias_from(S5TT[-1]["tb"], Cb)
            gt2 = [WS[:, 0, i * SEQ:(i + 1) * SEQ] for i in range(2)]
            gt2B = [Buf(), Buf()]
            alias_from(gt2B, [WSb[0]])
            TA_f32 = TA[:, :].bitcast(F32)
            TBL = []
            for st_ in range(2):
                base = st_ * 2048
                TBL.append(dict(ec=TA_f32[:, base:base + 512], es=TA_f32[:, base + 512:base + 1024],
                                ta=TA_f32[:, base + 1024:base + 1536], tb=TA_f32[:, base + 1536:base + 2048],
                                b=Buf()))
                alias_from([TBL[-1]["b"]], [TAb, SUb, dwwB, SMb] + P1bufs + parB_ab)
            carry = smalloc(8).rearrange("p (s c) -> p s c", s=2)
            carB = [Buf(), Buf()]
            ctmp = smalloc(4).rearrange("p (s c) -> p s c", s=2)
            GEL = [(SM[:, SMW - 1024:SMW - 512], SM[:, SMW - 512:SMW])] * 2
            GELB = [Buf()] * 2
            YPS = (0, 1)
            PSS = ((2, 3), (4, 5))
            GPS = (6, 7)
            d_gs = [dsrc(f"d_gs{i}") for i in range(2)]

            SL_ZS, SL_G, SL_V, SL_ZC = 32, 40, 48, 56
            scrB = [Buf() for _ in range(64)]
            mstg = smalloc(16).bitcast(BF16).rearrange("p (s c) -> p s c", s=2)
            mstB = [Buf(), Buf()]
            d_ms = [dsrc("d_ms0"), dsrc("d_ms1")]
            jobs = []
            for m in range(8):
                jobs.append(dict(slot=SL_ZS + m, col=C_ZS + 128 * m, func=AF.Silu, bias=None, meta=False))
            for m in range(8):
                jobs.append(dict(slot=SL_G + m, col=C_G + 128 * m, func=AF.Sigmoid, bias=None, meta=True))
            for m in range(8):
                jobs.append(dict(slot=SL_V + m, col=C_V + 128 * m, func=AF.Copy, bias=None, meta=True))
            for m in range(8):
                jobs.append(dict(slot=SL_ZC + m, col=C_ZC + 128 * m, func=AF.Silu, bias=None, meta=False))
            for j in range(32):
                jobs.append(dict(slot=j, col=C_GLC + 128 * j, func=AF.Sigmoid, bias=bgate_t[:, j:j + 1], meta=False))

            def proj_gen():
                nload = len(jobs) // 2

                def gload(li):
                    s_ = 1 + li % 3
                    wv_ = WS[:, s_, :].rearrange("p (k n) -> p k n", k=16)
                    c0 = jobs[2 * li]["col"]
                    load_w(w_in[:, c0:c0 + 256].rearrange("(k p) n -> p k n", p=128), [s_], wv_)
                    return s_, wv_
                nxt = gload(0)
                for ji, jb in enumerate(jobs):
                    if ji % 2 == 0:
                        wslot, wv = nxt
                        if ji // 2 + 1 < nload:
                            nxt = gload(ji // 2 + 1)
                    gsl = ji % 2
                    tiles = TT_ALL if jb["meta"] else TT_REAL
                    for tile in tiles:
                        t0, w = tile
                        bk = proj(wv, wslot, (ji % 2) * 128, tile, 16, xn_rhs, xn_bufs, pool=GPS)
                        if t0 == 0:
                            dst, dB = mstg[:, gsl, :], mstB[gsl]
                        else:
                            dst, dB = gt2[gsl][:, t0 - NM:t0 - NM + w], gt2B[gsl]
                        if jb["bias"] is None:
                            act(dst, ps[bk][:, 0:w], jb["func"], reads=(psB[bk],), writes=(dB,))
                        else:
                            act(dst, ps[bk][:, 0:w], jb["func"], bias=jb["bias"], reads=(psB[bk], SMb), writes=(dB,))
                        yield
                    if jb["meta"]:
                        dma("sp", ps_scr[jb["slot"]][:, 0:NM], mstg[:, gsl, :], d_ms[gsl], (mstB[gsl],),
                            (scrB[jb["slot"]],), nonc=True)
                    dma("sp", ps_scr[jb["slot"]][:, NM:], gt2[gsl], d_gs[gsl], (gt2B[gsl],), (scrB[jb["slot"]],))

            def s5_tables(T, q, part="all"):
                tb = (T["b"],)
                if part in ("all", 1):
                    act(T["ta"], iota_t, AF.Copy, scale=omf[:, q:q + 1], reads=(CONSTb, SMb), writes=tb)
                    act(T["tb"], iota_t, AF.Identity, bias=MAGIC, scale=omf[:, q:q + 1], reads=(CONSTb, SMb), writes=tb)
                    act(T["tb"], T["tb"], AF.Identity, bias=-MAGIC, reads=tb, writes=tb)
                if part in ("all", 2):
                    tt("dve", T["ta"], T["ta"], T["tb"], ALU.subtract, reads=tb, writes=tb)
                    act(T["es"], T["ta"], AF.Sin, scale=TWO_PI, reads=tb, writes=tb)
                    act(T["tb"], T["ta"], AF.Sin, scale=math.pi, reads=tb, writes=tb)
                    act(T["tb"], T["tb"], AF.Square, reads=tb, writes=tb)
                    act(T["ec"], T["tb"], AF.Identity, bias=1.0, scale=-2.0, reads=tb, writes=tb)

            def s5_stageA(sd, part="all"):
                blk, r4, ti, (t0, w), i = sd["blk"], sd["r4"], sd["ti"], sd["tile"], sd["i"]
                T = TBL[sd["pair"] % 2]
                tb = (T["b"],)
                Z = S5F[i % 3]
                f, fb = Z["f"], Z["fb"]
                pr, pi_ = PSS[i % 2]
                rp = 32 * r4
                ub = UBUF[rp:rp + 32, blk, t0:t0 + w]
                if part == "adds":
                    tt(S5ENG, f[0][:, 0:w], f[0][:, 0:w], f[1][:, 0:w], ALU.add, reads=(fb[0], fb[1]), writes=(fb[0],))
                    tt(S5ENG, f[2][:, 0:w], f[2][:, 0:w], f[3][:, 0:w], ALU.subtract, reads=(fb[2], fb[3]),
                       writes=(fb[2],))
                    return
                if part in ("all", "bmm"):
                    mm(ps[pr][:, 0:w], BT[rp:rp + 32, 0, blk, :], ub, True, True,
                       reads=(S5Wb, AbT[blk][ti]), writes=(psB[pr],), signal=True, tile_position=(rp, 0))
                    mm(ps[pi_][:, 0:w], BT[rp:rp + 32, 1, blk, :], ub, True, True,
                       reads=(S5Wb, AbT[blk][ti]), writes=(psB[pi_],), signal=True, tile_position=(rp, 0))
                    if part == "bmm":
                        return
                ec, es_ = T["ec"][:, 0:w], T["es"][:, 0:w]
                psW = ps2[pr // 2][:, :].rearrange("p (b n) -> p b n", b=2)[:, :, 0:w]
                ecb = ec.unsqueeze(1).broadcast_to([128, 2, w])
                esb = es_.unsqueeze(1).broadcast_to([128, 2, w])
                Pv = Z["P"].rearrange("p (b n) -> p b n", b=2)[:, :, 0:w]
                Qv = Z["Q"].rearrange("p (b n) -> p b n", b=2)[:, :, 0:w]
                if part != "modQ":
                    tt("dve", Pv, psW, ecb, ALU.mult, reads=(psB[pr], psB[pi_]) + tb, writes=(fb[0], fb[2]))
                    if part == "modP":
                        return
                tt("dve", Qv, psW, esb, ALU.mult, reads=(psB[pr], psB[pi_]) + tb, writes=(fb[3], fb[1]))
                if part in ("mods", "modQ"):
                    return
                tt(S5ENG, f[0][:, 0:w], f[0][:, 0:w], f[1][:, 0:w], ALU.add, reads=(fb[0], fb[1]), writes=(fb[0],))
                tt(S5ENG, f[2][:, 0:w], f[2][:, 0:w], f[3][:, 0:w], ALU.subtract, reads=(fb[2], fb[3]), writes=(fb[2],))

            def s5_stageB(sd, pending, part="all"):
                blk, r4, ti, (t0, w), i, q = sd["blk"], sd["r4"], sd["ti"], sd["tile"], sd["i"], sd["q"]
                T = TBL[sd["pair"] % 2]
                tb = (T["b"],)
                Z = S5F[i % 3]
                f, fb = Z["f"], Z["fb"]
                tbf, tbb = S5TT[i % 3]["t"], S5TT[i % 3]["tb"]
                rp = 32 * r4
                rq = mag[:, q:q + 1]
                cs = sd["pair"] % 2
                cB = carB[cs]
                ec, es_ = T["ec"][:, 0:w], T["es"][:, 0:w]
                if part in ("all", "scan"):
                    ini_re = 0.0 if ti == 0 else carry[:, cs, 0:1]
                    ini_im = 0.0 if ti == 0 else carry[:, cs, 1:2]
                    crd = () if ti == 0 else (cB,)
                    S.op("dve", lambda e: e.tensor_tensor_scan(
                        out=f[3][:, 0:w], data0=rq.broadcast_to([128, w]), data1=f[0][:, 0:w], initial=ini_re,
                        op0=ALU.mult, op1=ALU.add), (fb[0], SMb) + crd, (fb[3],))
                    S.op("dve", lambda e: e.tensor_tensor_scan(
                        out=f[1][:, 0:w], data0=rq.broadcast_to([128, w]), data1=f[2][:, 0:w], initial=ini_im,
                        op0=ALU.mult, op1=ALU.add), (fb[2], SMb) + crd, (fb[1],))
                if part in ("carryA", "carryB") and ti < len(TT_ALL) - 1:
                    CSq = (CS16 if w == 16 else CS512)[:, q, :]
                    NSq = (NS16 if w == 16 else NS512)[:, q, :]
                    sre_l, sim_l = f[3][:, w - 1:w], f[1][:, w - 1:w]
                    if part == "carryA":
                        tt("dve", ctmp[:, cs, 0:2], sim_l.broadcast_to([128, 2]), NSq, ALU.mult,
                           reads=(fb[1], SMb), writes=(cB,))
                    else:
                        stt(carry[:, cs, 0:2], CSq, sre_l, ctmp[:, cs, 0:2], ALU.mult, ALU.add,
                            reads=(fb[3], SMb, cB), writes=(cB,))
                if part in ("all", "demod") and ti > 0:
                    TT_ = S5TT[i % 3]
                    SQ = Z["Q"].rearrange("p (b n) -> p b n", b=2)[:, :, 0:w]
                    ecb = ec.unsqueeze(1).broadcast_to([128, 2, w])
                    esb = es_.unsqueeze(1).broadcast_to([128, 2, w])
                    tt("dve", TT_["P"].rearrange("p (b n) -> p b n", b=2)[:, :, 0:w], SQ, ecb, ALU.mult,
                       reads=(fb[3], fb[1]) + tb, writes=(tbb[0], tbb[2]))
                    tt("dve", TT_["Q"].rearrange("p (b n) -> p b n", b=2)[:, :, 0:w], SQ, esb, ALU.mult,
                       reads=(fb[3], fb[1]) + tb, writes=(tbb[3], tbb[1]))

                    def cpart(yb=YPS[i % 2]):
                        yo = ps[yb][rp:rp + 32, 0:w]
                        for ci_, wsel in enumerate((0, 1, 2, 2)):
                            mm(yo, WC[:, wsel, blk, rp:rp + 32], tbf[ci_][:, 0:w], ci_ == 0, False,
                               reads=(S5Wb, tbb[ci_]), writes=(psB[yb],), signal=False, tile_position=(0, rp))
                        mm(yo, DSKD[:, blk, rp:rp + 32], UBUF[:, blk, t0:t0 + w], False, True,
                           reads=(S5Wb, AbT[blk][ti]), writes=(psB[yb],), signal=True, tile_position=(0, rp))
                        cp("act", UBUF[rp:rp + 32, blk, t0:t0 + w], ps[yb][rp:rp + 32, 0:w],
                           reads=(psB[yb],), writes=(AbT[blk][ti],))
                    pending.append(cpart)

            def s5_gen():
                steps = []
                pair = 0
                for blk in range(8):
                    for r4 in range(4):
                        for ti, tile in enumerate(TT_ALL):
                            steps.append(dict(blk=blk, r4=r4, q=blk * 4 + r4, ti=ti, tile=tile, pair=pair, i=len(steps)))
                        pair += 1
                n = len(steps)
                pending = []
                epi_q = []
                s5_tables(TBL[0], 0)
                s5_stageA(steps[0])
                s5_stageA(steps[1], "bmm")
                for i in range(n):
                    sd = steps[i]
                    if sd["ti"] == 1 and sd["pair"] + 1 < 32:
                        s5_tables(TBL[(sd["pair"] + 1) % 2], sd["q"] + 1, 1)
                    if sd["ti"] == 2 and sd["pair"] + 1 < 32:
                        s5_tables(TBL[(sd["pair"] + 1) % 2], sd["q"] + 1, 2)
                    prev_pending = pending
                    pending = []
                    s5_stageB(sd, pending, "scan")
                    if i + 1 < n:
                        s5_stageA(steps[i + 1], "modP")
                    s5_stageB(sd, pending, "carryA")
                    if i + 1 < n:
                        s5_stageA(steps[i + 1], "modQ")
                    s5_stageB(sd, pending, "carryB")
                    for fn_ in prev_pending:
                        fn_()
                    if i + 2 < n:
                        s5_stageA(steps[i + 2], "bmm")
                    s5_stageB(sd, pending, "demod")
                    if i + 1 < n:
                        s5_stageA(steps[i + 1], "adds")
                    last_of_block = (sd["r4"] == 3 and sd["ti"] == len(TT_ALL) - 1)
                    if last_of_block:
                        for fn_ in pending:
                            fn_()
                        pending = []
                        blk = sd["blk"]
                        for ti_e, tile_e in enumerate(TT_REAL):
                            def _e1(blk=blk, ti_e=ti_e, tile_e=tile_e):
                                t0, w = tile_e
                                xg = UBUF[:, blk, t0:t0 + w]
                                act(GEL[0][0], xg, AF.Square, reads=(AbT[blk][ti_e + 1],), writes=(GELB[0],))
                                act(GEL[0][0], GEL[0][0], AF.Identity, bias=1.0, scale=0.044715, reads=(GELB[0],),
                                    writes=(GELB[0],))

                            def _e2(blk=blk, ti_e=ti_e, tile_e=tile_e):
                                t0, w = tile_e
                                xg = UBUF[:, blk, t0:t0 + w]
                                tt("dve", GEL[0][1], GEL[0][0], xg, ALU.mult, reads=(AbT[blk][ti_e + 1], GELB[0]),
                                   writes=(GELB[0],))
                                act(GEL[0][1], GEL[0][1], AF.Sigmoid, scale=GELU_C, reads=(GELB[0],), writes=(GELB[0],))

                            def _e3(blk=blk, ti_e=ti_e, tile_e=tile_e):
                                t0, w = tile_e
                                xg = UBUF[:, blk, t0:t0 + w]
                                tt("dve", xg, xg, GEL[0][1], ALU.mult, reads=(AbT[blk][ti_e + 1], GELB[0]),
                                   writes=(AbT[blk][ti_e + 1],))
                            epi_q.extend([_e1, _e2, _e3])
                    elif epi_q:
                        epi_q.pop(0)()
                    yield
                while epi_q:
                    epi_q.pop(0)()

            AbT = [[Buf() for _ in range(5)] for _ in range(8)]
            for b_ in range(8):
                for t_ in range(5):
                    AbT[b_][t_].w = Ab[b_].w
                    AbT[b_][t_].r = list(Ab[b_].r)
            g_s5 = s5_gen()
            g_pj = proj_gen()
            alive_pj = True
            stepn = 0
            for _ in g_s5:
                nun = (1, 2, 2, 2, 1 if (stepn // 5) % 2 else 2)[stepn % 5]
                stepn += 1
                for _u in range(nun):
                    if alive_pj:
                        try:
                            next(g_pj)
                        except StopIteration:
                            alive_pj = False
            if alive_pj:
                for _ in g_pj:
                    pass
            for b_ in range(8):
                Ab[b_].w = AbT[b_][4].w
                Ab[b_].r = [ev_ for t_ in range(5) for ev_ in ([AbT[b_][t_].w] if AbT[b_][t_].w else []) + AbT[b_][t_].r]

            if dbg is not None and dbg["what"] == "ys0":
                dbg_dump(8, lambda m: UBUF[:, m, :], lambda m: (Ab[m],), TA[:, 0:2 * L].bitcast(F32),
                         [TBL[0]["b"], TBL[1]["b"]])


            alias_from(WSb, gt2B)
            wg_v = WS[:, 0:2, :].rearrange("p s n -> p (s n)").rearrange("p (k n) -> p k n", k=8)
            load_w(w_glu.rearrange("(k p) n -> p k n", p=128), [0, 1], wg_v)
            ws_rr[0] = 2
            alias_from(Cb, [b_ for Z_ in S5F for b_ in Z_["fb"]] + [b_ for Z_ in S5TT for b_ in Z_["tb"]])
            ggt = CBUF[:, :, 0:1024].rearrange("p b (s t) -> p s b t", s=2)
            ggB = [Buf(), Buf()]
            alias_from(ggB, Cb)
            ys_rhs = lambda k, t0, w: UBUF[:, k, t0:t0 + w]
            ys_bufs = lambda k: (Ab[k],)
            for ti, tile in enumerate(TT_REAL):
                t0, w = tile
                gs_ = ti % 2
                for m in range(8):
                    bk = next_bank()
                    for k in range(8):
                        mm(ps[bk][:, 0:w], wg_v[:, k, m * 128:(m + 1) * 128], UBUF[:, k, t0:t0 + w], k == 0, k == 7,
                           reads=(WSb[0], WSb[1], Ab[k]), writes=(psB[bk],), signal=(k == 7))
                    act(ggt[:, gs_, m, 0:w], ps[bk][:, 0:w], AF.Sigmoid, bias=bglu_t[:, m:m + 1],
                        reads=(psB[bk], SMb), writes=(ggB[gs_],))
                tt("dve", UBUF[:, :, t0:t0 + w], UBUF[:, :, t0:t0 + w], ggt[:, gs_, :, 0:w], ALU.mult,
                   reads=tuple(Ab) + (ggB[gs_],), writes=tuple(Ab))

            WSF = WS[:, :, :].rearrange("p s n -> p (s n)")
            RT = [WSF[:, i * L:(i + 1) * L] for i in range(6)]
            RTB = [Buf() for _ in range(6)]
            d_rt = [dsrc(f"d_rt{i}") for i in range(6)]
            rt_rr = [0]
            rt_n = [6]

            def reload(slot, real_only=True):
                i_ = rt_rr[0] % rt_n[0]
                rt_rr[0] += 1
                if real_only:
                    dma("sp", RT[i_][:, NM:], ps_scr[slot][:, NM:], d_rt[i_], (scrB[slot],), (RTB[i_],))
                else:
                    dma("sp", RT[i_], ps_scr[slot], d_rt[i_], (scrB[slot],), (RTB[i_],))
                return RT[i_], RTB[i_]
            alias_from(RTB, WSb)
            zs_t = [reload(SL_ZS + m) for m in range(min(6, 8))]
            for m in range(8):
                zt, zB = zs_t[m] if m < 6 else reload(SL_ZS + m)
                tt("dve", UBUF[:, m, NM:], UBUF[:, m, NM:], zt[:, NM:], ALU.mult, reads=(Ab[m], zB), writes=(Ab[m],))

            if dbg is not None and dbg["what"] == "ys":
                dbg_dump(8, lambda m: UBUF[:, m, :], lambda m: (Ab[m],), C_f32[:, 0:L], list(Cb) + ggB)

            alias_from(Cb, ggB)
            XMF = XM[:, :, :].rearrange("p k t -> p (k t)")
            ABUF2 = XMF[:, 0:A_EL].rearrange("p (b t) -> p b t", b=8)
            X_f32 = XMF[:, A_EL:16 * L].bitcast(F32)
            xm_all_bufs = [b_ for k_ in range(16) for b_ in XMB[k_]]
            A2b = [Buf() for _ in range(8)]
            alias_from(A2b, xm_all_bufs)
            S.op("pool", lambda e: e.memset(ABUF2[:, :, 0:PADL], 0.0), (), tuple(A2b))
            cacc = [X_f32[:, 0:512], X_f32[:, 512:1024]]
            caccB = [Buf(), Buf()]
            alias_from(caccB, xm_all_bufs)
            sgB = [Buf(), Buf()]
            alias_from(sgB, [TBL[0]["b"], TBL[1]["b"], TAb])
            NDT = 8
            NPT = CW - NDT
            DG = [TA[:, 1024 + i * NPT * 128: 1024 + (i + 1) * NPT * 128].rearrange("p (k n) -> p k n", k=NPT)
                  for i in range(2)]
            DGB = [Buf(), Buf()]
            alias_from(DGB, [TBL[0]["b"], TBL[1]["b"], TAb])
            gv_t = {0: (reload(SL_G + 0, False), reload(SL_V + 0, False))}
            for m in range(8):
                if m + 1 < 8:
                    gv_t[m + 1] = (reload(SL_G + m + 1, False), reload(SL_V + m + 1, False))
                (sg_t, sg_B), (v_t, v_B) = gv_t.pop(m)
                tt("dve", ABUF2[:, m, PADL:PADL + L], v_t, sg_t, ALU.mult, reads=(v_B, sg_B), writes=(A2b[m],))
                for k in range(NDT, CW):
                    act(DG[m % 2][:, k - NDT, :], ident[:, :], AF.Copy, scale=dww_t[:, m, k:k + 1],
                        reads=(CONSTb, SMb), writes=(DGB[m % 2],))
                for tp_ in range(0, len(TT_REAL), 2):
                    pair_tiles = TT_REAL[tp_:tp_ + 2]
                    bks = []
                    for (t0, w) in pair_tiles:
                        bk = next_bank()
                        bks.append(bk)
                        for k in range(NDT, CW):
                            mm(ps[bk][:, 0:w], DG[m % 2][:, k - NDT, :], ABUF2[:, m, t0 + k:t0 + k + w], k == NDT,
                               k == CW - 1, reads=(DGB[m % 2], A2b[m]), writes=(psB[bk],), signal=(k == CW - 1))
                    for ai_, (t0, w) in enumerate(pair_tiles):
                        ts("dve", cacc[ai_][:, 0:w], ABUF2[:, m, t0:t0 + w], dww_t[:, m, 0:1], dwb_t[:, m:m + 1],
                           ALU.mult, ALU.add, reads=(A2b[m], SMb), writes=(caccB[ai_],))
                    for k in range(1, NDT):
                        for ai_, (t0, w) in enumerate(pair_tiles):
                            stt(cacc[ai_][:, 0:w], ABUF2[:, m, t0 + k:t0 + k + w], dww_t[:, m, k:k + 1],
                                cacc[ai_][:, 0:w], ALU.mult, ALU.add, reads=(A2b[m], SMb, caccB[ai_]),
                                writes=(caccB[ai_],))
                    for ai_, (t0, w) in enumerate(pair_tiles):
                        tt("dve", CBUF[:, m, t0:t0 + w], ps[bks[ai_]][:, 0:w], cacc[ai_][:, 0:w], ALU.add,
                           reads=(psB[bks[ai_]], caccB[ai_]), writes=(Cb[m],))

            if dbg is not None and dbg["what"] == "a":
                dbg_dump(8, lambda m: ABUF2[:, m, PADL:], lambda m: (A2b[m],), C_f32[:, 0:L], list(Cb))

            if dbg is not None and dbg["what"] == "dg":
                DGf = DG[0].rearrange("p k n -> p (k n)")
                views = [DGf[:, 0:L], DGf[:, 3968 - L:3968], dww_t.rearrange("p b k -> p (b k)")]
                d_dbg = dsrc("d_dbg")
                tmpf = C_f32[:, 0:L]
                evs = []
                for m_ in range(3):
                    wdt_ = views[m_].shape[1]
                    cp("dve", tmpf[:, 0:wdt_], views[m_], reads=(DGB[0], SMb), writes=tuple(Cb))
                    evs.append(dma("sp", dbg_out[m_][:, 0:wdt_], tmpf[:, 0:wdt_], d_dbg, tuple(Cb), ()))
                S.wait_all("sp", evs[-1:])
                dbg_finish(None)

            if dbg is not None and dbg["what"] == "conv":
                dbg_dump(8, lambda m: CBUF[:, m, :], lambda m: (Cb[m],), A_f32[:, 0:L], Ab)

            LNT = [Buf() for _ in range(4)]
            alias_from(LNT, xm_all_bufs + caccB)
            LNb = LNT[0]
            meanT = X_f32[:, 0:L]
            rstdT = X_f32[:, L:2 * L]
            sq_tmp = XMF[:, A_EL + 4 * L: A_EL + 4 * L + 4096].rearrange("p (b t) -> p b t", b=8)
            ln_t = [None, None] + [X_f32[:, 2 * L + 2048 + i * 512: 2 * L + 2048 + (i + 1) * 512] for i in range(2)]
            sqB = Buf()
            alias_from([sqB], xm_all_bufs)
            lnB = [Buf() for _ in range(4)]
            alias_from(lnB, xm_all_bufs)
            for ti, tile in enumerate(TT_REAL):
                t0, w = tile
                act(sq_tmp[:, :, 0:w], CBUF[:, :, t0:t0 + w], AF.Square, reads=tuple(Cb), writes=(sqB,))
                b1 = next_bank()
                for k in range(8):
                    mm(ps[b1][:, 0:w], ones_b[:, :], CBUF[:, k, t0:t0 + w], k == 0, k == 7,
                       reads=(CONSTb, Cb[k]), writes=(psB[b1],), signal=(k == 7))
                b2 = next_bank()
                for k in range(8):
                    mm(ps[b2][:, 0:w], ones_b[:, :], sq_tmp[:, k, 0:w], k == 0, k == 7,
                       reads=(CONSTb, sqB), writes=(psB[b2],), signal=(k == 7))
                mt = meanT[:, t0:t0 + w]
                rt = rstdT[:, t0:t0 + w]
                lt_ = (LNT[ti],)
                S.op("act", lambda e, mt=mt, b1=b1, w=w: e.mul(out=mt, in_=ps[b1][:, 0:w], mul=1.0 / E),
                     (psB[b1],), lt_)
                tt("dve", rt, mt, mt, ALU.mult, reads=lt_, writes=lt_)
                stt(rt, ps[b2][:, 0:w], 1.0 / E, rt, ALU.mult, ALU.subtract, reads=(psB[b2],) + lt_, writes=lt_)
                act(rt, rt, AF.Sqrt, bias=LN_EPS, reads=lt_, writes=lt_)
                S.op("dve", lambda e, rt=rt: e.reciprocal(out=rt, in_=rt), lt_, lt_)

            rt_n[0] = 3
            rt_rr[0] = 0

            ln4 = [X_f32[:, 4128:4640], X_f32[:, 4640:5152], ln_t[2], ln_t[3]]
            alias_from(lnB, [sqB])

            def p2g_gen():
                zc_t = {0: reload(SL_ZC + 0), 1: reload(SL_ZC + 1)}
                deferred = []
                grp = 0
                for m in range(8):
                    for fn_ in deferred:
                        fn_()
                    deferred = []
                    if m + 2 < 8:
                        zc_t[m + 2] = reload(SL_ZC + m + 2)
                    zt, zB = zc_t.pop(m)
                    for tp_ in range(0, 4, 2):
                        tiles_ = TT_REAL[tp_:tp_ + 2]
                        sets_ = [(grp % 2) * 2, (grp % 2) * 2 + 1]
                        grp += 1
                        for (t0, w), si in zip(tiles_, sets_):
                            tt("dve", ln4[si][:, 0:w], CBUF[:, m, t0:t0 + w], meanT[:, t0:t0 + w], ALU.subtract,
                               reads=(Cb[m], LNT[tidx(t0) - 1]), writes=(lnB[si],))
                        for (t0, w), si in zip(tiles_, sets_):
                            tt("dve", ln4[si][:, 0:w], ln4[si][:, 0:w], rstdT[:, t0:t0 + w], ALU.mult,
                               reads=(lnB[si], LNT[tidx(t0) - 1]), writes=(lnB[si],))
                        for (t0, w), si in zip(tiles_, sets_):
                            act(ln4[si][:, 0:w], ln4[si][:, 0:w], AF.Silu, bias=lnb_t[:, m:m + 1], scale=lng_t[:, m:m + 1],
                                reads=(lnB[si], SMb), writes=(lnB[si],))
                        for fn_ in deferred:
                            fn_()
                        deferred = []
                        for (t0, w), si in zip(tiles_, sets_):
                            deferred.append(lambda t0=t0, w=w, si=si, m=m, zt=zt, zB=zB: tt(
                                "dve", CBUF[:, m, t0:t0 + w], ln4[si][:, 0:w], zt[:, t0:t0 + w], ALU.mult,
                                reads=(zB, lnB[si]), writes=(Cb[m],)))
                        yield
                for fn_ in deferred:
                    fn_()

            alias_from([WSb[2]], [RTB[3], RTB[4], RTB[5]])
            alias_from([WSb[3]], [RTB[5]])
            YSB = Ab
            YS = UBUF
            for k_ in range(16):
                alias_from(XMB[k_], A2b if k_ < 8 else (A2b + LNT + [sqB] + lnB + caccB))
            gtl = [TA[:, i * SEQ:(i + 1) * SEQ] for i in range(4)]
            gtB = [Buf() for _ in range(4)]
            alias_from(gtB, sgB + DGB)
            d_gl = [dsrc(f"d_gl{i}") for i in range(4)]
            mtmp = [SM[:, SMW - 1024:SMW - 512], SM[:, SMW - 512:SMW]]
            mtB = [Buf(), Buf()]
            gt_rr = [0]

            def gate_load(slot):
                i_ = gt_rr[0] % 4
                gt_rr[0] += 1
                dma("sp", gtl[i_], ps_scr[slot][:, NM:], d_gl[i_], (scrB[slot],), (gtB[i_],))
                return gtl[i_], gtB[i_]

            def wload(wsrc, c0, slot):
                v_ = WS[:, slot, :].rearrange("p (k n) -> p k n", k=8)
                load_w(wsrc[:, c0:c0 + 512].rearrange("(k p) n -> p k n", p=128), [slot], v_)
                return v_, slot

            def part(j, wv_, wslot_, lc, src, srcB, gate, first):
                g_t, g_B = gate
                for ti, tile in enumerate(TT_REAL):
                    t0, w = tile
                    bk = next_bank()
                    for k in range(8):
                        mm(ps[bk][:, 0:w], wv_[:, k, lc:lc + 128], src[:, k, t0:t0 + w], k == 0, k == 7,
                           reads=(WSb[wslot_], srcB[k]), writes=(psB[bk],), signal=(k == 7))
                    xb = XMB[j][tidx(t0)]
                    if first:
                        tt("dve", XM[:, j, t0:t0 + w], ps[bk][:, 0:w], g_t[:, t0 - NM:t0 - NM + w], ALU.mult,
                           reads=(psB[bk], g_B), writes=(xb,))
                    else:
                        ms = ti % 2
                        tt("dve", mtmp[ms][:, 0:w], ps[bk][:, 0:w], g_t[:, t0 - NM:t0 - NM + w], ALU.mult,
                           reads=(psB[bk], g_B), writes=(mtB[ms],))
                        tt("dve", XM[:, j, t0:t0 + w], XM[:, j, t0:t0 + w], mtmp[ms][:, 0:w], ALU.add,
                           reads=(xb, mtB[ms]), writes=(xb,))
                    yield

            def ssm_lo_gen():
                wA = wload(w_ssm, 0, 2)
                wB = wload(w_ssm, 512, 3)
                for j in range(8):
                    wv_, ws_ = wA if j < 4 else wB
                    yield from part(j, wv_, ws_, (j % 4) * 128, YS, YSB, gate_load(16 + j), True)

            g1_, g2_ = p2g_gen(), ssm_lo_gen()
            a1_, a2_ = True, True
            while a1_ or a2_:
                if a1_:
                    try:
                        next(g1_)
                    except StopIteration:
                        a1_ = False
                for _r in range(2):
                    if a2_:
                        try:
                            next(g2_)
                        except StopIteration:
                            a2_ = False

            if dbg is not None and dbg["what"] == "yc":
                dbg_dump(8, lambda m: CBUF[:, m, :], lambda m: (Cb[m],), A_f32[:, 0:L], list(Ab) + [LNb, sqB] + lnB)

            def drain(g_):
                for _ in g_:
                    pass
            alias_from([WSb[0]], [RTB[0], RTB[1]])
            alias_from([WSb[1]], [RTB[1], RTB[2], RTB[3]])
            wA = wload(w_conv, 0, 0)
            wB = wload(w_conv, 512, 1)
            for j in range(8):
                wv_, ws_ = wA if j < 4 else wB
                drain(part(j, wv_, ws_, (j % 4) * 128, CBUF, Cb, gate_load(j), False))
            for jg in range(2):
                c0 = 1024 + 512 * jg
                wS = wload(w_ssm, c0, 2 if jg == 0 else 0)
                wC = wload(w_conv, c0, 3 if jg == 0 else 1)
                for j in range(8 + 4 * jg, 12 + 4 * jg):
                    lc = (j % 4) * 128
                    drain(part(j, wS[0], wS[1], lc, YS, YSB, gate_load(16 + j), True))
                    drain(part(j, wC[0], wC[1], lc, CBUF, Cb, gate_load(j), False))

            if dbg is not None and dbg["what"] == "merged":
                dbg_dump(16, lambda m: XM[:, m, :], lambda m: xm_all(m), A_f32[:, 0:L], YSB)

            WOb = [[Buf(), Buf()] for _ in range(4)]
            alias_from([b_ for pr_ in WOb for b_ in pr_], YSB + Cb)
            WO = AC[:, 0:16 * D].rearrange("p (k n) -> p k n", k=16)
            d_wo = [[dsrc(f"d_wo{i}_{h}") for h in range(2)] for i in range(4)]
            for n in range(4):
                for h in range(2):
                    dma("pool", WO[:, 8 * h:8 * h + 8, n * 512:(n + 1) * 512],
                        w_out[1024 * h:1024 * (h + 1), n * 512:(n + 1) * 512].rearrange("(k p) n -> p k n", p=128),
                        d_wo[n][h], (), (WOb[n][h],))
            fgB = Buf()
            alias_from([fgB], gtB)
            fg = TA_f32[:, 0:D]
            dma("sp", fg, final_g.partition_broadcast(128), d_par_new(), (), (fgB,), nonc=True)
            WS_f32 = WS[:, :, :].rearrange("p s n -> p (s n)").bitcast(F32)
            xt4 = [WS_f32[:, 0:D], WS_f32[:, D:2 * D]]
            ht4 = [WS_f32[:, 2 * D:3 * D], WS_f32[:, 3 * D:4 * D]]
            x4B = [Buf(), Buf()]
            h4B = [Buf(), Buf()]
            alias_from(x4B + h4B, WSb)
            junk4 = TA[:, 4096:4096 + D]
            jB = Buf()
            alias_from([jB], gtB)
            st4 = smalloc(4)
            s4B = [Buf(), Buf()]
            d_x4 = [dsrc("d_x4a"), dsrc("d_x4b")]
            d_o = [dsrc("d_o0"), dsrc("d_o1")]
            out_events = []
            dma("sp", xt4[0], x[0:128, :], d_x4[0], (), (x4B[0],))
            for i in range(16):
                s_ = i % 2
                tok0 = NM + 128 * i
                if i + 1 < 16:
                    dma("sp", xt4[1 - s_], x[128 * (i + 1):128 * (i + 2), :], d_x4[1 - s_], (), (x4B[1 - s_],))
                for n in range(4):
                    bk = next_bank()
                    for k in range(16):
                        mm(ps[bk][:, :], XM[:, k, tok0:tok0 + 128], WO[:, k, n * 512:(n + 1) * 512], k == 0, k == 15,
                           reads=(XMB[k][tidx(tok0)], WOb[n][k // 8]), writes=(psB[bk],), signal=(k == 15))
                    tt("dve", ht4[s_][:, n * 512:(n + 1) * 512], ps[bk][:, :], xt4[s_][:, n * 512:(n + 1) * 512], ALU.add,
                       reads=(psB[bk], x4B[s_]), writes=(h4B[s_],))
                act(junk4, ht4[s_], AF.Square, accum=st4[:, s_:s_ + 1], reads=(h4B[s_],), writes=(jB, s4B[s_]))
                act(st4[:, s_:s_ + 1], st4[:, s_:s_ + 1], AF.Sqrt, bias=NORM_EPS, scale=1.0 / D,
                    reads=(s4B[s_],), writes=(s4B[s_],))
                S.op("dve", lambda e, s_=s_: e.reciprocal(out=st4[:, 2 + s_:3 + s_], in_=st4[:, s_:s_ + 1]),
                     (s4B[s_],), (s4B[s_],))
                stt(ht4[s_], ht4[s_], st4[:, 2 + s_:3 + s_], fg, ALU.mult, ALU.mult,
                    reads=(h4B[s_], s4B[s_], fgB), writes=(h4B[s_],))
                ev = dma("sp", out[128 * i:128 * (i + 1), :], ht4[s_], d_o[s_], (h4B[s_],), ())
                out_events.append(ev)
            S.wait_all("sp", out_events[-2:])

        try:
            program()
        except _Stop:
            pass

        with nc.allow_non_contiguous_dma(reason="small parameter layouts"):
            with nc.Block() as block:
                @block.sync
                def _(e):
                    S.emit("sp", e)

                @block.tensor
                def _(e):
                    S.emit("pe", e)

                @block.scalar
                def _(e):
                    S.emit("act", e)

                @block.vector
                def _(e):
                    S.emit("dve", e)

                @block.gpsimd
                def _(e):
                    S.emit("pool", e)
    return nc


_PARAMS = ["meta", "norm_g", "w_in", "b_gate", "dw_w", "dw_b", "ln_g", "ln_b", "w_conv", "lam_re", "lam_im",
           "log_step", "b_re", "b_im", "c_re", "c_im", "d_skip", "w_glu", "b_glu", "w_ssm", "w_out", "final_g"]


def kernel(**inputs):
    xs = np.ascontiguousarray(np.asarray(inputs["x"], dtype=np.float32))
    params = {k: np.ascontiguousarray(np.asarray(inputs[k], dtype=np.float32)) for k in _PARAMS}
    nc = build()
    in_maps = []
    for c in range(NCORES):
        m = dict(params)
        m["x"] = xs[c]
        in_maps.append(m)
    res = run_bass_kernel_spmd(nc, in_maps, core_ids=list(range(NCORES)))
    return np.stack([np.asarray(r["out"], dtype=np.float32) for r in res.results], axis=0)
```
